# Optimizing a Trainium2 kernel written in Bass

```python
import functools
import jax, jax.numpy as jnp
from jax import lax
import numpy as np

D_MODEL = 1024
BATCH = 16
SEQ = 4096
DEPTH = 2
DEC_BATCH = 16
DEC_SEQ = 64
PAST_LEN = 1024

CHUNK = 64
M_HEADS = 4
M_HDIM = D_MODEL // 4
M_WIDTH = M_HEADS * M_HDIM
A_HEADS = 8
A_HDIM = 64
A_WIDTH = A_HEADS * A_HDIM
PAST_CHUNKS = 8
ATT_REACH = PAST_CHUNKS * CHUNK
BAND = ATT_REACH + CHUNK
REL_CLIP = 128
R_HEADS = 8
N_KEYS = 128
N_EXPERTS = N_KEYS * N_KEYS
KEY_DIM = 256
HALF_KEY = KEY_DIM // 2
TOPK = 16
PEER_BLOCK = 256
PLE_DIM = 256
ALPHA = (2 * DEPTH) ** 0.25
BETA = (8 * DEPTH) ** -0.25
LN_EPS = 1e-5

SPLITS = (M_WIDTH, M_WIDTH, M_WIDTH, M_HEADS, M_HEADS, M_WIDTH,
          A_WIDTH, A_WIDTH, A_WIDTH,
          D_MODEL, D_MODEL)
IN_COLS = sum(SPLITS)
SPLIT_IDX = tuple(int(s) for s in np.cumsum(SPLITS)[:-1])

kernel_name = 'hybrid_mlstm_bandattn_peer_stream_step'


def layer_norm(x, g, b):
    xf = x.astype(jnp.float32)
    mu = xf.mean(-1, keepdims=True)
    var = jnp.square(xf - mu).mean(-1, keepdims=True)
    return ((xf - mu) * lax.rsqrt(var + LN_EPS) * g.astype(jnp.float32) + b.astype(jnp.float32)).astype(x.dtype)


def head_norm(h, g):
    mu = h.mean(-1, keepdims=True)
    var = jnp.square(h - mu).mean(-1, keepdims=True)
    return (h - mu) * lax.rsqrt(var + LN_EPS) * g.astype(jnp.float32)


def mlstm_chunk(state, inp):
    C, n, m = (t.astype(jnp.float32) for t in state)
    q, k, v, ig, lf = (t.astype(jnp.float32) for t in inp)
    L = q.shape[1]
    b = jnp.cumsum(lf, axis=1)
    inter = b + m[:, None, :]
    causal = jnp.tril(jnp.ones((L, L), dtype=bool))
    intra = b[:, :, None, :] - b[:, None, :, :] + ig[:, None, :, :]
    intra = jnp.where(causal[None, :, :, None], intra, -jnp.inf)
    m_t = jnp.maximum(inter, intra.max(axis=2))
    w_inter = jnp.exp(inter - m_t)
    w_intra = jnp.exp(intra - m_t[:, :, None, :])
    a = w_intra * jnp.einsum('bthd,bshd->btsh', q, k)
    num = w_inter[..., None] * jnp.einsum('bhvk,bthk->bthv', C, q) + jnp.einsum('btsh,bshv->bthv', a, v)
    den = w_inter * jnp.einsum('bhk,bthk->bth', n, q) + a.sum(axis=2)
    h = num / jnp.maximum(jnp.abs(den), jnp.exp(-m_t))[..., None]
    m_new = m_t[:, -1]
    g_state = jnp.exp(b[:, -1] + m - m_new)
    g_s = jnp.exp(b[:, -1:, :] - b + ig - m_new[:, None, :])
    C_new = g_state[..., None, None] * C + jnp.einsum('bsh,bshv,bshk->bhvk', g_s, v, k)
    n_new = g_state[..., None] * n + jnp.einsum('bsh,bshk->bhk', g_s, k)
    return (C_new, n_new, m_new), h


def mlstm_prompt(q, k, v, ig, lf):
    B, S = q.shape[:2]
    nc = S // CHUNK
    to_chunks = lambda t: jnp.swapaxes(t.reshape((B, nc, CHUNK) + t.shape[2:]), 0, 1)
    init = (jnp.zeros((B, M_HEADS, M_HDIM, M_HDIM), jnp.float32),
            jnp.zeros((B, M_HEADS, M_HDIM), jnp.float32),
            jnp.zeros((B, M_HEADS), jnp.float32))
    state, h = lax.scan(mlstm_chunk, init, tuple(to_chunks(t) for t in (q, k, v, ig, lf)))
    return jnp.swapaxes(h, 0, 1).reshape(B, S, M_HEADS, M_HDIM), state


def mlstm_sample(q, k, v, ig, lf, C, n, m):
    state, h = mlstm_chunk((C, n, m), (q, k, v, ig, lf))
    return h, state


def band_attention(q, k, v, q_pos, k_pos, rel_bias):
    qc = q_pos // CHUNK
    kc = k_pos // CHUNK
    visible = (k_pos[None, :] >= 0) & (kc[None, :] <= qc[:, None]) & (kc[None, :] >= qc[:, None] - PAST_CHUNKS)
    rel = jnp.clip(q_pos[:, None] - k_pos[None, :], -REL_CLIP, REL_CLIP) + REL_CLIP
    bias = rel_bias.astype(jnp.float32)[:, rel]
    s = jnp.einsum('bqhd,bkhd->bhqk', q, k).astype(jnp.float32) * (A_HDIM ** -0.5) + bias[None]
    s = jnp.where(visible[None, None], s, -jnp.inf)
    p = jax.nn.softmax(s, axis=-1)
    return jnp.einsum('bhqk,bkhd->bqhd', p.astype(v.dtype), v)


def attn_prompt(q, k, v, rel_bias):
    B, S = q.shape[:2]
    nc = S // CHUNK
    pad = ((0, 0), (ATT_REACH, 0), (0, 0), (0, 0))
    kp, vp = jnp.pad(k, pad), jnp.pad(v, pad)

    def one_chunk(c):
        start = c * CHUNK
        qb = lax.dynamic_slice_in_dim(q, start, CHUNK, axis=1)
        kb = lax.dynamic_slice_in_dim(kp, start, BAND, axis=1)
        vb = lax.dynamic_slice_in_dim(vp, start, BAND, axis=1)
        q_pos = start + jnp.arange(CHUNK)
        k_pos = start - ATT_REACH + jnp.arange(BAND)
        return band_attention(qb, kb, vb, q_pos, k_pos, rel_bias)

    o = lax.map(one_chunk, jnp.arange(nc))
    o = jnp.swapaxes(o, 0, 1).reshape(B, S, A_HEADS, A_HDIM)
    keep = min(ATT_REACH, S)
    return o, k[:, S - keep:], v[:, S - keep:]


def attn_sample(q, k, v, rel_bias, cache_k, cache_v):
    n_cache = cache_k.shape[1]
    L = q.shape[1]
    kc = jnp.concatenate([cache_k.astype(k.dtype), k], axis=1)
    vc = jnp.concatenate([cache_v.astype(v.dtype), v], axis=1)
    q_pos = PAST_LEN + jnp.arange(L)
    k_pos = PAST_LEN - n_cache + jnp.arange(n_cache + L)
    return band_attention(q, kc, vc, q_pos, k_pos, rel_bias), k, v


def token_mixer(x, w_in, if_bias, m_norm, rel_bias, w_bm, w_ba, w_o, mlstm_core, attn_core):
    B, L, _ = x.shape
    z = x @ w_in
    mq, mk, mv, mi, mf, mo, aq, ak, av, gm, ga = jnp.split(z, SPLIT_IDX, axis=-1)
    mh = lambda t: t.reshape(B, L, M_HEADS, M_HDIM)
    ah = lambda t: t.reshape(B, L, A_HEADS, A_HDIM)
    ig = (mi + if_bias[0]).astype(jnp.float32)
    lf = jax.nn.log_sigmoid((mf + if_bias[1]).astype(jnp.float32))
    h, mstate = mlstm_core(mh(mq), mh(mk) * (M_HDIM ** -0.5), mh(mv), ig, lf)
    h = head_norm(h, m_norm).astype(x.dtype).reshape(B, L, M_WIDTH)
    y_m = (jax.nn.sigmoid(mo) * h) @ w_bm
    o_a, k_new, v_new = attn_core(ah(aq), ah(ak), ah(av), rel_bias)
    y_a = o_a.reshape(B, L, A_WIDTH) @ w_ba
    mix = (jax.nn.sigmoid(gm) * y_m + jax.nn.sigmoid(ga) * y_a) @ w_o
    return mix, mstate, (k_new, v_new)


def peer_ffn(x, wq, subkeys, u_tab, v_tab):
    B, L, D = x.shape
    T = B * L
    nblk = -(-T // PEER_BLOCK)
    xt = jnp.pad(x.reshape(T, D), ((0, nblk * PEER_BLOCK - T), (0, 0))).reshape(nblk, PEER_BLOCK, D)

    def block(xb):
        qr = (xb @ wq).reshape(PEER_BLOCK, R_HEADS, 2, HALF_KEY)
        s = jnp.einsum('thpd,hpnd->thpn', qr, subkeys).astype(jnp.float32)
        sv, si = lax.top_k(s, TOPK)
        cand = (sv[:, :, 0, :, None] + sv[:, :, 1, None, :]).reshape(PEER_BLOCK, R_HEADS, TOPK * TOPK)
        cidx = (si[:, :, 0, :, None] * N_KEYS + si[:, :, 1, None, :]).reshape(PEER_BLOCK, R_HEADS, TOPK * TOPK)
        fv, fi = lax.top_k(cand, TOPK)
        eidx = jnp.take_along_axis(cidx, fi, axis=-1)
        g = jax.nn.softmax(fv, axis=-1)
        act = jax.nn.gelu(jnp.einsum('thkd,td->thk', u_tab[eidx], xb).astype(jnp.float32))
        w = (g * act).astype(xb.dtype)
        return jnp.einsum('thk,thkd->td', w, v_tab[eidx])

    return lax.map(block, xt).reshape(nblk * PEER_BLOCK, D)[:T].reshape(B, L, D)


def encoder_layer(x, pe, lw, mlstm_core, attn_core):
    (w_in, if_b, m_norm, rel_b, w_bm, w_ba, w_o, ln1g, ln1b, ln2g, ln2b,
     pq, psk, pu, pv, plp, plg) = lw
    mix, mstate, kv = token_mixer(x, w_in, if_b, m_norm, rel_b, w_bm, w_ba, w_o, mlstm_core, attn_core)
    x = layer_norm(ALPHA * x + mix, ln1g, ln1b)
    x = layer_norm(ALPHA * x + peer_ffn(x, pq, psk, pu, pv), ln2g, ln2b)
    x = x + (pe @ plp) * jax.nn.sigmoid(x @ plg)
    return x, mstate, kv


def setup_inputs(seed: int = 0) -> dict:
    key = jax.random.key(seed)
    ks = iter(jax.random.split(key, 40))
    nrm = lambda shape, scale: jax.random.normal(next(ks), shape, jnp.float32) * scale
    att_cache = min(ATT_REACH, PAST_LEN)
    f_bias = jnp.broadcast_to(jnp.linspace(3.0, 6.0, M_HEADS), (DEPTH, M_HEADS)) + nrm((DEPTH, M_HEADS), 0.1)
    i_bias = nrm((DEPTH, M_HEADS), 0.1)
    return {
        'x_prompt': nrm((BATCH, SEQ, D_MODEL), 1.0),
        'x_sample': nrm((DEC_BATCH, DEC_SEQ, D_MODEL), 1.0),
        'cache_attn_k': nrm((DEPTH, DEC_BATCH, att_cache, A_HEADS, A_HDIM), 1.0),
        'cache_attn_v': nrm((DEPTH, DEC_BATCH, att_cache, A_HEADS, A_HDIM), 1.0),
        'state_mlstm_C': nrm((DEPTH, DEC_BATCH, M_HEADS, M_HDIM, M_HDIM), 0.5),
        'state_mlstm_n': nrm((DEPTH, DEC_BATCH, M_HEADS, M_HDIM), 0.5),
        'state_mlstm_m': nrm((DEPTH, DEC_BATCH, M_HEADS), 1.0),
        'p_prompt': nrm((DEPTH, BATCH, SEQ, PLE_DIM), 1.0),
        'p_sample': nrm((DEPTH, DEC_BATCH, DEC_SEQ, PLE_DIM), 1.0),
        'w_in': nrm((DEPTH, D_MODEL, IN_COLS), D_MODEL ** -0.5),
        'mlstm_if_bias': jnp.stack([i_bias, f_bias], axis=1),
        'mlstm_norm_w': 1.0 + nrm((DEPTH, M_HEADS, M_HDIM), 0.05),
        'attn_rel_bias': nrm((DEPTH, A_HEADS, 2 * REL_CLIP + 1), 0.5),
        'w_branch_m': nrm((DEPTH, M_WIDTH, D_MODEL), BETA * M_WIDTH ** -0.5),
        'w_branch_a': nrm((DEPTH, A_WIDTH, D_MODEL), BETA * A_WIDTH ** -0.5),
        'w_out': nrm((DEPTH, D_MODEL, D_MODEL), BETA * D_MODEL ** -0.5),
        'ln1_g': 1.0 + nrm((DEPTH, D_MODEL), 0.05),
        'ln1_b': nrm((DEPTH, D_MODEL), 0.02),
        'ln2_g': 1.0 + nrm((DEPTH, D_MODEL), 0.05),
        'ln2_b': nrm((DEPTH, D_MODEL), 0.02),
        'peer_wq': nrm((DEPTH, D_MODEL, R_HEADS * KEY_DIM), D_MODEL ** -0.5),
        'peer_subkeys': nrm((DEPTH, R_HEADS, 2, N_KEYS, HALF_KEY), HALF_KEY ** -0.5),
        'peer_u': nrm((DEPTH, N_EXPERTS, D_MODEL), D_MODEL ** -0.5),
        'peer_v': nrm((DEPTH, N_EXPERTS, D_MODEL), BETA * R_HEADS ** -0.5),
        'ple_proj': nrm((DEPTH, PLE_DIM, D_MODEL), BETA * PLE_DIM ** -0.5),
        'ple_gate': nrm((DEPTH, D_MODEL, D_MODEL), D_MODEL ** -0.5),
    }


def reference(x_prompt, x_sample, cache_attn_k, cache_attn_v, state_mlstm_C, state_mlstm_n,
              state_mlstm_m, p_prompt, p_sample, w_in, mlstm_if_bias, mlstm_norm_w, attn_rel_bias,
              w_branch_m, w_branch_a, w_out, ln1_g, ln1_b, ln2_g, ln2_b, peer_wq, peer_subkeys,
              peer_u, peer_v, ple_proj, ple_gate):
    layer_w = (w_in, mlstm_if_bias, mlstm_norm_w, attn_rel_bias, w_branch_m, w_branch_a, w_out,
               ln1_g, ln1_b, ln2_g, ln2_b, peer_wq, peer_subkeys, peer_u, peer_v, ple_proj, ple_gate)
    yp, ys = x_prompt, x_sample
    kp_l, vp_l, Cp_l, np_l, mp_l = [], [], [], [], []
    ks_l, vs_l, Cs_l, ns_l, ms_l = [], [], [], [], []
    for i in range(DEPTH):
        lw = tuple(w[i] for w in layer_w)
        yp, (Cp, n_p, m_p), (kp, vp) = encoder_layer(yp, p_prompt[i], lw, mlstm_prompt, attn_prompt)
        ys, (Cs, n_s, m_s), (ksm, vsm) = encoder_layer(
            ys, p_sample[i], lw,
            functools.partial(mlstm_sample, C=state_mlstm_C[i], n=state_mlstm_n[i], m=state_mlstm_m[i]),
            functools.partial(attn_sample, cache_k=cache_attn_k[i], cache_v=cache_attn_v[i]))
        kp_l.append(kp); vp_l.append(vp); Cp_l.append(Cp); np_l.append(n_p); mp_l.append(m_p)
        ks_l.append(ksm); vs_l.append(vsm); Cs_l.append(Cs); ns_l.append(n_s); ms_l.append(m_s)
    st = lambda l: jnp.stack(l, axis=0)
    return (yp, ys, st(kp_l), st(vp_l), st(Cp_l), st(np_l), st(mp_l),
            st(ks_l), st(vs_l), st(Cs_l), st(ns_l), st(ms_l))
```

```python
import contextlib
import numpy as np
import concourse.bass as bass
import concourse.mybir as mybir
from concourse.bass_utils import run_bass_kernel_spmd

F32 = mybir.dt.float32
BF16 = mybir.dt.bfloat16
I32 = mybir.dt.int32
U32 = mybir.dt.uint32
AF = mybir.ActivationFunctionType
ALU = mybir.AluOpType
AX = mybir.AxisListType

D = 1024
INC = 7688
DEPTH = 2
ALPHA = (2 * DEPTH) ** 0.25
LN_EPS = 1e-5
NEG = -30000.0
ENGS = ("pe", "dve", "act", "pool", "sp")
GS = 4
NGB = 3
NSLAB = 2
SLABW = 1032


class Prog:
    def __init__(self, nc, es, same_engine_sync=True):
        self.nc = nc
        self.es = es
        self.q = {e: [] for e in ENGS}
        self.cnt = {e: 0 for e in ENGS}
        self.semh = {}
        for e in ENGS:
            self.semh[e] = es.enter_context(nc.semaphore("s_" + e))
        self.known = {e: {} for e in ENGS}
        self.lastw = {}
        self.readers = {}
        self.dcount = {}
        self.dcls = {}
        self.groupsem = set()
        self.same = same_engine_sync

    def sb(self, name, shape, dt):
        return self.es.enter_context(self.nc.sbuf_tensor(name, list(shape), dt))

    def ps(self, name, shape, dt=F32):
        return self.es.enter_context(self.nc.psum_tensor(name, list(shape), dt))

    def _deps(self, eng, reads, writes, nowaw=False):
        need = {}

        def add(ev):
            if ev is None:
                return
            k, v = ev
            if k in self.groupsem:
                v = self.dcount[k]
            if need.get(k, 0) < v:
                need[k] = v

        for r in reads:
            add(self.lastw.get(r))
        for w in writes:
            if not nowaw:
                add(self.lastw.get(w))
            for k, v in self.readers.get(w, {}).items():
                add((k, v))
        waits = []
        for k, v in need.items():
            if k == eng and (eng == "pe" or not self.same):
                continue
            if self.known[eng].get(k, 0) < v:
                self.known[eng][k] = v
                waits.append((k, v))
        return waits

    def _record(self, ev, reads, writes):
        k, v = ev
        for w in writes:
            self.lastw[w] = ev
            self.readers[w] = {}
        for r in reads:
            d = self.readers.setdefault(r, {})
            if d.get(k, 0) < v:
                d[k] = v

    @staticmethod
    def _l(x):
        return [x] if isinstance(x, str) else list(x)

    def op(self, eng, fn, reads=(), writes=()):
        reads, writes = self._l(reads), self._l(writes)
        pr = [r for r in reads if r.startswith("ps")]
        if pr:
            writes = writes + [r for r in pr if r not in writes]
            reads = [r for r in reads if not r.startswith("ps")]
        waits = self._deps(eng, reads, writes)
        self.cnt[eng] += 1
        val = self.cnt[eng]
        sem = self.semh[eng]
        semh = self.semh

        def thunk(e):
            for k, v in waits:
                e.wait_ge(semh[k], v)
            fn(e).then_inc(sem, 1)

        self.q[eng].append(thunk)
        self._record((eng, val), reads, writes)

    def dma(self, eng, cls, fn, reads=(), writes=(), ring=1, group=False, nowaw=False):
        reads, writes = self._l(reads), self._l(writes)
        st = self.dcls.setdefault(cls, {"i": 0, "keys": []})
        if len(st["keys"]) < ring:
            key = "%s#%d" % (cls, len(st["keys"]))
            st["keys"].append(key)
            self.semh[key] = self.es.enter_context(self.nc.semaphore("d_" + key.replace("#", "_")))
            self.dcount[key] = 0
            if group:
                self.groupsem.add(key)
        key = st["keys"][st["i"] % ring]
        st["i"] += 1
        waits = self._deps(eng, reads, writes, nowaw=nowaw)
        prev = self.dcount[key]
        if not group and prev > 0 and self.known[eng].get(key, 0) < prev:
            self.known[eng][key] = prev
            waits.append((key, prev))
        self.dcount[key] += 16
        val = self.dcount[key]
        sem = self.semh[key]
        semh = self.semh

        def thunk(e):
            for k, v in waits:
                e.wait_ge(semh[k], v)
            fn(e).then_inc(sem, 16)

        self.q[eng].append(thunk)
        self._record((key, val), reads, writes)

    def final_wait(self, eng="sp"):
        waits = []
        for k in self.semh:
            v = self.cnt[k] if k in ENGS else self.dcount[k]
            if k != eng and v > 0:
                waits.append((k, v))
        semh = self.semh

        def thunk(e):
            for k, v in waits:
                e.wait_ge(semh[k], v)

        self.q[eng].append(thunk)

    def emit(self):
        nc = self.nc
        q = self.q
        with nc.Block() as block:

            @block.tensor
            def _(e):
                for t in q["pe"]:
                    t(e)

            @block.vector
            def _(e):
                for t in q["dve"]:
                    t(e)

            @block.scalar
            def _(e):
                for t in q["act"]:
                    t(e)

            @block.gpsimd
            def _(e):
                for t in q["pool"]:
                    t(e)

            @block.sync
            def _(e):
                for t in q["sp"]:
                    t(e)


def build(NP, SEQ, NS, debug=None, limit=None, nocast=False):
    nc = bass.Bass("TRN2", target_bir_lowering=False)
    NT = SEQ // 128
    assert SEQ % 128 == 0 and NT >= 4

    def din(name, shape, dt=F32):
        return nc.dram_tensor(name, list(shape), dt, kind="ExternalInput").ap()

    def dout(name, shape, dt=F32):
        return nc.dram_tensor(name, list(shape), dt, kind="ExternalOutput").ap()

    def dint(name, shape, dt=F32):
        return nc.dram_tensor(name, list(shape), dt, kind="Internal").ap()

    xp = din("xp", [NP, SEQ, D]); xs = din("xs", [NS, 64, D])
    ck = din("ck", [DEPTH, NS, 512, 512]); cv = din("cv", [DEPTH, NS, 512, 512])
    sC = din("sC", [DEPTH, NS, 4, 256, 256]); sn = din("sn", [DEPTH, NS, 4, 256]); sm = din("sm", [DEPTH, NS, 4])
    pp = din("pp", [DEPTH, NP, SEQ, 256]); psm = din("psm", [DEPTH, NS, 64, 256])
    w_in = din("w_in", [DEPTH, D, INC]); ifb = din("ifb", [DEPTH, 8]); mnw = din("mnw", [DEPTH, 1024])
    rb = din("rb", [DEPTH, 8, 257])
    wbm = din("wbm", [DEPTH, 1024, 1024]); wba = din("wba", [DEPTH, 512, 1024]); wo = din("wo", [DEPTH, 1024, 1024])
    ln1g = din("ln1g", [DEPTH, D]); ln1b = din("ln1b", [DEPTH, D]); ln2g = din("ln2g", [DEPTH, D]); ln2b = din("ln2b", [DEPTH, D])
    wq = din("wq", [DEPTH, 1024, 2048]); sk = din("sk", [DEPTH, 16, 128, 128])
    pu = din("pu", [DEPTH, 16384, 1024]); pv = din("pv", [DEPTH, 16384, 1024])
    plp = din("plp", [DEPTH, 256, 1024]); plg = din("plg", [DEPTH, 1024, 1024])

    yp = dout("yp", [NP, SEQ, D]); ys = dout("ys", [NS, 64, D])
    kp_o = dout("kp", [DEPTH, NP, 512, 512]); vp_o = dout("vp", [DEPTH, NP, 512, 512])
    Cp_o = dout("Cp", [DEPTH, NP, 4, 256, 256]); np_o = dout("np", [DEPTH, NP, 4, 256]); mp_o = dout("mp", [DEPTH, NP, 4])
    ks_o = dout("ks", [DEPTH, NS, 64, 512]); vs_o = dout("vs", [DEPTH, NS, 64, 512])
    Cs_o = dout("Cs", [DEPTH, NS, 4, 256, 256]); ns_o = dout("ns", [DEPTH, NS, 4, 256]); ms_o = dout("ms", [DEPTH, NS, 4])

    xmp = dint("xmp", [NP, SEQ, D]); xms = dint("xms", [NS, 64, D])
    w_in_b = dint("w_in_b", [DEPTH, D, INC], BF16)
    wbm_b = dint("wbm_b", [DEPTH, 1024, 1024], BF16); wba_b = dint("wba_b", [DEPTH, 512, 1024], BF16)
    wo_b = dint("wo_b", [DEPTH, 1024, 1024], BF16); wq_b = dint("wq_b", [DEPTH, 1024, 2048], BF16)
    plp_b = dint("plp_b", [DEPTH, 256, 1024], BF16); plg_b = dint("plg_b", [DEPTH, 1024, 1024], BF16)
    ub = [dint(f"ub{l}", [16384, 1024], BF16) for l in range(DEPTH)]
    vb = [dint(f"vb{l}", [16384, 1024], BF16) for l in range(DEPTH)]
    rbx = dint("rbx", [DEPTH, 8, 384])
    dbg_outs = {}

    with contextlib.ExitStack() as es:
        p = Prog(nc, es)

        def mm(out, lhsT, rhs, start, stop, r, w):
            p.op("pe", lambda e: e.matmul(out, lhsT=lhsT, rhs=rhs, start=start, stop=stop), r, w)

        def tr(out, in_, ident, r, w):
            p.op("pe", lambda e: e.transpose(out=out, in_=in_, identity=ident), r, w)

        def act(out, in_, func, r, w, bias=None, scale=None):
            kw = {}
            if bias is not None:
                kw["bias"] = bias
            if scale is not None:
                kw["scale"] = scale
            p.op("act", lambda e: e.activation(out=out, in_=in_, func=func, **kw), r, w)

        def acopy(out, in_, r, w):
            p.op("act", lambda e: e.copy(out=out, in_=in_), r, w)

        def vcopy(out, in_, r, w, eng="dve"):
            p.op(eng, lambda e: e.tensor_copy(out=out, in_=in_), r, w)

        def tt(out, in0, in1, op, r, w, eng="dve"):
            p.op(eng, lambda e: e.tensor_tensor(out=out, in0=in0, in1=in1, op=op), r, w)

        def ts(out, in0, s1, s2, op0, op1, r, w, eng="dve"):
            if s2 is None:
                p.op(eng, lambda e: e.tensor_scalar(out=out, in0=in0, scalar1=s1, scalar2=None, op0=op0), r, w)
            else:
                p.op(eng, lambda e: e.tensor_scalar(out=out, in0=in0, scalar1=s1, scalar2=s2, op0=op0, op1=op1), r, w)

        def stt(out, in0, scalar, in1, op0, op1, r, w, accum=None, eng="dve"):
            if accum is None:
                p.op(eng, lambda e: e.scalar_tensor_tensor(out=out, in0=in0, scalar=scalar, in1=in1, op0=op0, op1=op1), r, w)
            else:
                p.op(eng, lambda e: e.scalar_tensor_tensor(out=out, in0=in0, scalar=scalar, in1=in1, op0=op0, op1=op1,
                                                          accum_out=accum), r, w)

        def tsc(out, in_, scalar, op, r, w):
            p.op("dve", lambda e: e.tensor_single_scalar(out=out, in_=in_, scalar=scalar, op=op), r, w)

        def red(out, in_, op, r, w):
            p.op("dve", lambda e: e.tensor_reduce(out=out, in_=in_, axis=AX.X, op=op), r, w)

        def recip(out, in_, r, w):
            p.op("dve", lambda e: e.reciprocal(out=out, in_=in_), r, w)

        def memset(ap, val, w, eng="pool", r=()):
            p.op(eng, lambda e: e.memset(ap, val), r, w)

        RING = {"setup": 8, "st": 8, "castw": 8, "castt0": 4, "castt1": 4}

        def dma(eng, semkey, out, in_, r, w):
            p.dma(eng, semkey, lambda e: e.dma_start(out=out, in_=in_), r, w, ring=RING.get(semkey, 1))

        def dump(name, ap, res, shape):
            if debug is None or name not in debug or name in dbg_outs:
                return
            o = dout("dbg_" + name, shape)
            dbg_outs[name] = shape
            dma("act", "st", o, ap, [res], [])

        slabs = [p.sb(f"wsl{i}", [128, 8, SLABW], BF16) for i in range(NSLAB)]
        gbs = [p.sb(f"gb{i}", [128, GS, 1024], BF16) for i in range(NGB)]
        lng1 = p.sb("lng1", [128, D], F32); lnb1 = p.sb("lnb1", [128, D], F32)
        lng2 = p.sb("lng2", [128, D], F32); lnb2 = p.sb("lnb2", [128, D], F32)
        xb = p.sb("xb", [128, D], F32)
        fA = p.sb("fA", [128, D], F32)
        fB = p.sb("fB", [128, D], F32)
        x1 = p.sb("x1", [128, D], F32)
        sc = p.sb("sc", [128, 2048], F32)
        sc2 = [p.sb(f"sc2_{i}", [128, 128], F32) for i in range(2)]
        cand2 = [p.sb(f"cand2_{i}", [128, 256], F32) for i in range(2)]
        top = p.sb("top", [128, 16, 16], F32)
        idxu = p.sb("idxu", [128, 16, 16], U32)
        sif = p.sb("sif", [128, 16, 16], F32)
        fv = p.sb("fv", [128, 8, 16], F32)
        fiu = p.sb("fiu", [128, 8, 16], U32)
        fau = p.sb("fau", [128, 8, 16], U32)
        fbu = p.sb("fbu", [128, 8, 16], U32)
        faf = p.sb("faf", [128, 8, 16], F32)
        fbf = p.sb("fbf", [128, 8, 16], F32)
        sel0 = p.sb("sel0", [128, 8, 16], F32)
        sel1 = p.sb("sel1", [128, 8, 16], F32)
        eidxf = p.sb("eidxf", [128, 128], F32)
        eidxi = p.sb("eidxi", [128, 128], I32)
        gw = p.sb("gw", [128, 8, 16], F32)
        gsum = p.sb("gsum", [128, 8], F32)
        dots = p.sb("dots", [128, 128], F32)
        ge1 = p.sb("ge1", [128, 128], F32)
        ge2 = p.sb("ge2", [128, 128], F32)
        wgt = p.sb("wgt", [128, 128], F32)
        xT = p.sb("xT", [128, 8, 128], BF16)
        bq = p.sb("bq", [128, 1024], BF16)
        bk = p.sb("bk", [128, 1024], BF16)
        bkt = p.sb("bkt", [128, 1024], BF16)
        vg = p.sb("vg", [128, 4, 257], BF16)
        sgmo = p.sb("sgmo", [128, 8, 128], BF16)
        sggm = p.sb("sggm", [128, 8, 128], BF16)
        sgga = p.sb("sgga", [128, 8, 128], BF16)
        bh = p.sb("bh", [128, 1024], BF16)
        mixT = p.sb("mixT", [128, 8, 128], BF16)
        qrT = p.sb("qrT", [128, 16, 128], BF16)
        aqT = p.sb("aqT", [128, 4, 128], BF16)
        akT = [p.sb(f"akT{i}", [128, 4, 128], BF16) for i in range(5)]
        avr = [p.sb(f"avr{i}", [128, 8, 65], BF16) for i in range(5)]
        Caug = p.sb("Caug", [128, 4, 2, 257], F32)
        Csb = [p.sb(f"Csb{i}", [128, 2, 257], BF16) for i in range(2)]
        ATb = [p.sb(f"AT{i}", [128, 128], BF16) for i in range(2)]
        BT0 = p.sb("BT0", [128, 8, 128], F32)
        BT1 = p.sb("BT1", [128, 8, 128], F32)
        identf = p.sb("identf", [128, 128], F32)
        identb = p.sb("identb", [128, 128], BF16)
        onesf = p.sb("onesf", [128, 128], F32)
        maskU = p.sb("maskU", [128, 128], F32)
        Jm = p.sb("Jm", [128, 128], F32)
        mask0 = p.sb("mask0", [128, 128], F32)
        mask4 = p.sb("mask4", [128, 128], F32)
        io16 = p.sb("io16", [128, 16], F32)
        atmp = [p.sb(f"atmp{i}", [128, 128], F32) for i in range(2)]
        PTb = [p.sb(f"PT{i}", [128, 5, 128], BF16) for i in range(2)]
        oa = p.sb("oa", [128, 8, 64], F32)
        oaT = p.sb("oaT", [128, 4, 128], BF16)
        tmpf = p.sb("tmpf", [128, 4, 128], F32)
        peb = p.sb("peb", [128, 256], F32)
        peT = p.sb("peT", [128, 2, 128], BF16)
        dgm = [p.sb(f"dgm{i}", [128, GS, 128], BF16) for i in range(2)]
        kst = p.sb("kst", [128, 512], F32)
        vst = p.sb("vst", [128, 512], F32)
        cio = p.sb("cio", [128, 2, 256], F32)
        skT = p.sb("skT", [128, 16, 128], BF16)
        ifbb = p.sb("ifbb", [128, 8], F32)
        st8 = p.sb("st8", [8, 128], F32)
        dg8 = p.sb("dg8", [8, 8], F32)
        rbs = p.sb("rbs", [8, 384], F32)
        nst = p.sb("nst", [128, 8], F32)
        mnc = p.sb("mnc", [128, 8], F32)
        cb = p.sb("cb", [128, 8], F32)
        z8 = p.sb("z8", [128, 8], F32)
        e4 = p.sb("e4", [128, 4], F32)
        sp4 = p.sb("sp4", [128, 4], F32)
        rcs = p.sb("rcs", [128, 8], F32)
        rmax = p.sb("rmax", [4, 1], F32)
        dg4 = p.sb("dg4", [4, 4], F32)
        Mxb = p.sb("Mxb", [128, 4], F32)
        mb = p.sb("mb", [128, 4], F32)
        D12 = p.sb("D12", [128, 12], F32)
        E12 = p.sb("E12", [128, 12], F32)
        den = p.sb("den", [128, 2], F32)
        bst = p.sb("bst", [128, 4, 6], F32)
        bmv = p.sb("bmv", [128, 4, 2], F32)
        rstd = p.sb("rstd", [128, 4], F32)
        rs8 = p.sb("rs8", [128, 8], F32)

        PS = [p.ps(f"ps{i}", [128, 512], F32) for i in range(8)]
        PQ = [(PS[0], "ps0"), (PS[1], "ps1")]
        PTR, PSM, PY, PA1, PX0, PX1 = PS[2], PS[3], PS[4], PS[5], PS[6], PS[7]
        pq_i = [0]

        def next_pq():
            b = PQ[pq_i[0] % 2]
            pq_i[0] += 1
            return b

        qT = bq[:].rearrange("p (c l) -> p c l", c=8)
        kT = bk[:].rearrange("p (c l) -> p c l", c=8)
        x1T = kT
        hnT = bh[:].rearrange("p (c l) -> p c l", c=8)
        x2T = hnT
        m1 = fB[:].rearrange("p (c l) -> p c l", c=8)
        hbuf = fA[:].rearrange("p (h d) -> p h d", h=4)

        memset(identf[:], 1.0, ["identf"])
        p.op("pool", lambda e: e.affine_select(out=identf[:], in_=identf[:], pattern=[[-1, 128]], compare_op=ALU.is_equal,
                                              fill=0.0, base=0, channel_multiplier=1), ["identf"], ["identf"])
        memset(onesf[:], 1.0, ["onesf"])
        memset(maskU[:], 1.0, ["maskU"])
        p.op("pool", lambda e: e.affine_select(out=maskU[:], in_=maskU[:], pattern=[[1, 128]], compare_op=ALU.is_ge,
                                              fill=0.0, base=0, channel_multiplier=-1), ["maskU"], ["maskU"])
        memset(Jm[:], 1.0, ["Jm"])
        p.op("pool", lambda e: e.affine_select(out=Jm[:], in_=Jm[:], pattern=[[1, 128]], compare_op=ALU.is_equal,
                                              fill=0.0, base=-127, channel_multiplier=1), ["Jm"], ["Jm"])
        memset(mask0[:], 0.0, ["mask0"])
        memset(mask0[64:128, 0:64], NEG, ["mask0"], r=["mask0"])
        memset(mask4[:], 0.0, ["mask4"])
        memset(mask4[0:64, 64:128], NEG, ["mask4"], r=["mask4"])
        for j in range(16):
            memset(io16[:, j:j + 1], float(j), ["io16"], r=["io16"])
        vcopy(identb[:], identf[:], ["identf"], ["identb"])
        memset(D12[:], 0.0, ["D12"])
        memset(dots[:], 0.0, ["dots"])
        for i in range(5):
            memset(avr[i][:], 1.0, [f"avr{i}"])
            memset(akT[i][:], 0.0, [f"akT{i}"])
        memset(vg[:], 0.0, ["vg"])
        memset(bkt[:], 0.0, ["bkt"])

        wnames = {}

        def cast(semkey, dst, src, nm):
            lst = wnames.setdefault(nm, [])
            name = "%s#%d" % (nm, len(lst))
            lst.append(name)
            dma("pool", semkey, dst, src, [], [name])

        for l in range(DEPTH if nocast in (False, 0, 2) else 0):
            for r0 in range(0, 1024, 128):
                cast("castw", w_in_b[l, r0:r0 + 128, :], w_in[l, r0:r0 + 128, :], f"w_in_b{l}")
            for (dst, src, nm, rows) in ((wbm_b, wbm, "wbm_b", 1024), (wba_b, wba, "wba_b", 512), (wo_b, wo, "wo_b", 1024),
                                         (wq_b, wq, "wq_b", 1024), (plp_b, plp, "plp_b", 256), (plg_b, plg, "plg_b", 1024)):
                for r0 in range(0, rows, 256):
                    cast("castw", dst[l, r0:r0 + 256, :], src[l, r0:r0 + 256, :], f"{nm}{l}")
        for l in range(DEPTH if nocast in (False, 0, 3) else 0):
            pu2 = pu[l].rearrange("(a b) d -> a (b d)", b=8)
            pv2 = pv[l].rearrange("(a b) d -> a (b d)", b=8)
            ub2 = ub[l].rearrange("(a b) d -> a (b d)", b=8)
            vb2 = vb[l].rearrange("(a b) d -> a (b d)", b=8)
            for r0 in range(0, 2048, 128):
                cast(f"castt{l}", ub2[r0:r0 + 128, :], pu2[r0:r0 + 128, :], f"ub{l}")
                cast(f"castt{l}", vb2[r0:r0 + 128, :], pv2[r0:r0 + 128, :], f"vb{l}")

        stages = []

        def stage(src, kc, ncols, srcres, fn):
            stages.append((src, kc, ncols, srcres, fn))

        slab_ctr = [0]

        def load_slab(st):
            src, kc, ncols, srcres, _ = st
            if src is None:
                return None
            s = slab_ctr[0] % NSLAB
            slab_ctr[0] += 1
            t = slabs[s]
            dma("sp", f"wsl{s}", t[:, 0:kc, 0:ncols], src, wnames[srcres], [f"wsl{s}"])
            return (t, f"wsl{s}")

        def wslab(wb, l, kcn, c0, c1):
            return wb[l].rearrange("(kc p) n -> p kc n", p=128)[:, :, c0:c1]

        def proj_fm(slab, c0, nchunks, kcn, rhsT, rres, L, evac):
            sl, sres = slab
            for g0 in range(0, nchunks, 4):
                n = min(4, nchunks - g0)
                bank, bres = next_pq()
                for j in range(n):
                    for kc in range(kcn):
                        mm(bank[:, j * 128:j * 128 + L], sl[:, kc, c0 + (g0 + j) * 128:c0 + (g0 + j + 1) * 128],
                           rhsT[:, kc, :L], kc == 0, kc == kcn - 1, [sres, rres], [bres])
                view = bank[:, 0:n * 128].rearrange("p (j l) -> p j l", l=128)[:, :, :L]
                evac(view, bres, g0, n)

        def proj_tm(slab, c0, ncols, kcn, lhsT, lres, L, evac, banks=None):
            sl, sres = slab
            bi = 0
            for n0 in range(0, ncols, 512):
                n = min(512, ncols - n0)
                if banks is None:
                    bank, bres = next_pq()
                else:
                    bank, bres = banks[bi]
                    bi += 1
                for kc in range(kcn):
                    mm(bank[:L, 0:n], lhsT[:, kc, :L], sl[:, kc, c0 + n0:c0 + n0 + n], kc == 0, kc == kcn - 1,
                       [sres, lres], [bres])
                evac(bank, bres, n0, n)

        def transpose_to(dst3, dres, src2, sres, L, ncols, evac_eng="act"):
            nch = ncols // 128
            for g0 in range(0, nch, 4):
                n = min(4, nch - g0)
                for j in range(n):
                    tr(PTR[:, j * 128:j * 128 + L], src2[:L, (g0 + j) * 128:(g0 + j + 1) * 128], identf[:L, :L],
                       [sres, "identf"], ["ps2"])
                view = PTR[:, 0:n * 128].rearrange("p (j l) -> p j l", l=128)[:, :, :L]
                if evac_eng == "act":
                    acopy(dst3[:, g0:g0 + n, :L], view, ["ps2"], [dres])
                else:
                    vcopy(dst3[:, g0:g0 + n, :L], view, ["ps2"], [dres])

        def layer_norm(src, sres, dst, dres, g_b, b_b, L):
            p.op("dve", lambda e: e.bn_stats(out=bst[:L, 0, :], in_=src[:L, 0:512]), [sres], ["bst"])
            p.op("dve", lambda e: e.bn_stats(out=bst[:L, 1, :], in_=src[:L, 512:1024]), [sres], ["bst"])
            p.op("dve", lambda e: e.bn_aggr(out=bmv[:L, 0, :], in_=bst[:L, 0:2, :]), ["bst"], ["bmv"])
            act(rstd[:L, 0:1], bmv[:L, 0, 1:2], AF.Ln, ["bmv"], ["rstd"], bias=LN_EPS)
            act(rstd[:L, 0:1], rstd[:L, 0:1], AF.Exp, ["rstd"], ["rstd"], scale=-0.5)
            ts(dst[:L, :], src[:L, :], bmv[:L, 0, 0:1], rstd[:L, 0:1], ALU.subtract, ALU.mult, [sres, "bmv", "rstd"], [dres])
            tt(dst[:L, :], dst[:L, :], g_b[:L, :], ALU.mult, [dres, "lnc"], [dres])
            tt(dst[:L, :], dst[:L, :], b_b[:L, :], ALU.add, [dres, "lnc"], [dres])

        def layer_setup(l):
            dma("sp", "setup", ifbb[:], ifb[l:l + 1, :].to_broadcast([128, 8]), [], ["ifbb"])
            dma("sp", "setup", st8[0:8, :], mnw[l].rearrange("(c p) -> c p", p=128), [], ["st8"])
            tr(PSM[:, 300:308], st8[0:8, :], identf[0:8, 0:8], ["st8", "identf"], ["ps3"])
            vcopy(mnc[:, :], PSM[:, 300:308], ["ps3"], ["mnc"])
            dma("sp", "setup", rbs[0:8, 0:257], rb[l], [], ["rbs"])
            vcopy(rbs[0:8, 257:384], rbs[0:8, 256:257].to_broadcast([8, 127]), ["rbs"], ["rbs"])
            dma("sp", "setup", rbx[l], rbs[0:8, :], ["rbs"], [f"rbx{l}"])
            dma("sp", "setup", lng1[:], ln1g[l:l + 1, :].to_broadcast([128, D]), [], ["lnc"])
            dma("sp", "setup", lnb1[:], ln1b[l:l + 1, :].to_broadcast([128, D]), [], ["lnc"])
            dma("sp", "setup", lng2[:], ln2g[l:l + 1, :].to_broadcast([128, D]), [], ["lnc"])
            dma("sp", "setup", lnb2[:], ln2b[l:l + 1, :].to_broadcast([128, D]), [], ["lnc"])
            ts(dg8[0:8, 0:8], identf[0:8, 0:8], rbs[0:8, 256:257], None, ALU.mult, None, ["identf", "rbs"], ["dg8"])
            mm(PSM[:, 310:318], onesf[0:8, :], dg8[0:8, 0:8], True, True, ["onesf", "dg8"], ["ps3"])
            vcopy(cb[:, :], PSM[:, 310:318], ["ps3"], ["cb"])
            sc3 = sc[:].rearrange("p (g n) -> p g n", g=16)
            dma("sp", "setup", sc3, sk[l].rearrange("g n d -> n g d"), [], ["sc"])
            for g0 in range(0, 16, 4):
                for j in range(4):
                    tr(PTR[:, j * 128:(j + 1) * 128], sc3[:, g0 + j, :], identf[:], ["sc", "identf"], ["ps2"])
                acopy(skT[:, g0:g0 + 4, :], PTR[:].rearrange("p (j l) -> p j l", l=128), ["ps2"], ["skT"])
            for h in range(8):
                base = (l * 8 + h) * 384
                dma("sp", "setup", sc3[:, h, :], bass.AP(tensor=rbx.tensor, offset=base + 1, ap=[[1, 128], [1, 128]]),
                    [f"rbx{l}"], ["sc"])
                dma("sp", "setup", sc3[:, 8 + h, :], bass.AP(tensor=rbx.tensor, offset=base + 129, ap=[[1, 128], [1, 128]]),
                    [f"rbx{l}"], ["sc"])
            for h in range(8):
                bank, bres = next_pq()
                mm(bank[:, 0:128], Jm[:], sc3[:, h, :], True, True, ["Jm", "sc"], [bres])
                mm(bank[:, 128:256], Jm[:], sc3[:, 8 + h, :], True, True, ["Jm", "sc"], [bres])
                stt(BT0[:, h, :], bank[:, 0:128], cb[:, h:h + 1], mask0[:], ALU.subtract, ALU.add, [bres, "cb", "mask0"], ["BT0"])
                ts(BT1[:, h, :], bank[:, 128:256], cb[:, h:h + 1], None, ALU.subtract, None, [bres, "cb"], ["BT1"])

        def seq_start_prompt():
            memset(Caug[:], 0.0, ["Caug"], eng="dve")
            memset(mb[:], 0.0, ["mb"], eng="dve")

        def seq_start_sample(l, b):
            for h in range(4):
                dma("sp", "setup", cio[:], sC[l, b, h].rearrange("(vc p) k -> p vc k", p=128), [], ["cio"])
                for kc in range(2):
                    for vc in range(2):
                        tr(PTR[:, vc * 128:(vc + 1) * 128], cio[:, vc, kc * 128:(kc + 1) * 128], identf[:], ["cio", "identf"], ["ps2"])
                    acopy(Caug[:, h, kc, 0:256], PTR[:, 0:256], ["ps2"], ["Caug"])
            dma("sp", "setup", st8[0:8, :], sn[l, b].rearrange("h (c p) -> (h c) p", p=128), [], ["st8"])
            tr(PSM[:, 300:308], st8[0:8, :], identf[0:8, 0:8], ["st8", "identf"], ["ps3"])
            vcopy(Caug[:, :, :, 256], PSM[:, 300:308].rearrange("p (h c) -> p h c", h=4), ["ps3"], ["Caug"])
            dma("sp", "setup", mb[:], sm[l, b:b + 1, :].to_broadcast([128, 4]), [], ["mb"])
            for c in range(4):
                slot = (4 + c) % 5
                dma("sp", "setup", fA[:, 0:512], ck[l, b, c * 128:(c + 1) * 128, :], [], ["fA"])
                dma("sp", "setup", fA[:, 512:1024], cv[l, b, c * 128:(c + 1) * 128, :], [], ["fA"])
                for j in range(4):
                    tr(PTR[:, j * 128:(j + 1) * 128], fA[:, j * 128:(j + 1) * 128], identf[:], ["fA", "identf"], ["ps2"])
                acopy(akT[slot][:], PTR[:].rearrange("p (j l) -> p j l", l=128), ["ps2"], [f"akT{slot}"])
                vcopy(avr[slot][:, :, 0:64], fA[:, 512:1024].rearrange("p (h d) -> p h d", h=8), ["fA"], [f"avr{slot}"])

        def seq_end(l, C_dst, n_dst, m_dst):
            for h in range(4):
                for vc in range(2):
                    for kc in range(2):
                        tr(PTR[:, kc * 128:(kc + 1) * 128], Caug[:, h, kc, vc * 128:(vc + 1) * 128], identf[:],
                           ["Caug", "identf"], ["ps2"])
                    acopy(cio[:, vc, :], PTR[:, 0:256], ["ps2"], ["cio"])
                dma("act", "st", C_dst[h].rearrange("(vc p) k -> p vc k", p=128), cio[:], ["cio"], [])
            vcopy(nst[:, :].rearrange("p (h c) -> p h c", h=4), Caug[:, :, :, 256], ["Caug"], ["nst"])
            tr(PSM[0:8, 320:448], nst[:, 0:8], identf[:, :], ["nst", "identf"], ["ps3"])
            vcopy(st8[0:8, :], PSM[0:8, 320:448], ["ps3"], ["st8"])
            dma("act", "st", n_dst.rearrange("h (c p) -> (h c) p", p=128), st8[0:8, :], ["st8"], [])
            dma("act", "st", m_dst, mb[0:1, :], ["mb"], [])

        def tile(l, L, ti, xsrc, xres, xdst, xdres, pesrc, kdst, vdst):
            W = w_in_b

            def s_load():
                dma("sp", "xld", xb[:L, :], xsrc, [xres] if xres else [], ["xb"])
                dma("sp", "peld", peb[:L, :], pesrc, [], ["peb"])
                transpose_to(xT, "xT", xb, "xb", L, 1024)

            def s0(slab):
                s_load()

                def ev(view, bres, g0, n):
                    acopy(qT[:, g0:g0 + n, :L], view, [bres], ["bq"])
                proj_fm(slab, 0, 8, 8, xT, "xT", L, ev)
            stage(wslab(W, l, 8, 0, 1024), 8, 1024, f"w_in_b{l}", s0)

            def s1(slab):
                def ev(view, bres, g0, n):
                    act(kT[:, g0:g0 + n, :L], view, AF.Copy, [bres], ["bk"], scale=0.0625)
                proj_fm(slab, 0, 8, 8, xT, "xT", L, ev)

                def ev2(bank, bres, n0, n):
                    act(bkt[:L, n0:n0 + n], bank[:L, 0:n], AF.Copy, [bres], ["bkt"], scale=0.0625)
                proj_tm(slab, 0, 1024, 8, xT, "xT", L, ev2)
            stage(wslab(W, l, 8, 1024, 2048), 8, 1024, f"w_in_b{l}", s1)

            def s2(slab):
                sl, sres = slab
                for kc in range(8):
                    mm(PSM[:L, 0:8], xT[:, kc, :L], sl[:, kc, 1024:1032], kc == 0, kc == 7, [sres, "xT"], ["ps3"])
                tt(z8[:L, :], PSM[:L, 0:8], ifbb[:L, :], ALU.add, ["ps3", "ifbb"], ["z8"])
                act(e4[:L, :], z8[:L, 4:8], AF.Exp, ["z8"], ["e4"], scale=-1.0)
                act(sp4[:L, :], e4[:L, :], AF.Ln, ["e4"], ["sp4"], bias=1.0)
                mm(PSM[:L, 16:20], maskU[:L, :L], sp4[:L, :], True, True, ["maskU", "sp4"], ["ps3"])
                mm(PSM[:, 24:28], onesf[:L, :], sp4[:L, :], True, True, ["onesf", "sp4"], ["ps3"])
                tt(rcs[:L, 0:4], z8[:L, 0:4], PSM[:L, 16:20], ALU.add, ["z8", "ps3"], ["rcs"])
                vcopy(rcs[:L, 4:8], PSM[:L, 16:20], ["ps3"], ["rcs"])
                vcopy(sp4[:, :], PSM[:, 24:28], ["ps3"], ["sp4"])
                tr(PSM[0:4, 32:32 + L], rcs[:L, 0:4], identf[:L, :L], ["rcs", "identf"], ["ps3"])
                red(rmax[0:4, :], PSM[0:4, 32:32 + L], ALU.max, ["ps3"], ["rmax"])
                ts(dg4[:, :], identf[0:4, 0:4], rmax[0:4, 0:1], None, ALU.mult, None, ["identf", "rmax"], ["dg4"])
                mm(PSM[:, 200:204], onesf[0:4, :], dg4[:, :], True, True, ["onesf", "dg4"], ["ps3"])
                tt(Mxb[:, :], mb[:, :], PSM[:, 200:204], ALU.max, ["mb", "ps3"], ["Mxb"])
                tt(D12[:L, 0:4], rcs[:L, 0:4], Mxb[:L, :], ALU.subtract, ["rcs", "Mxb"], ["D12"])
                tt(D12[:, 4:8], mb[:, :], Mxb[:, :], ALU.subtract, ["mb", "Mxb"], ["D12"])
                tt(D12[:L, 8:12], rcs[:L, 4:8], Mxb[:L, :], ALU.subtract, ["rcs", "Mxb"], ["D12"])
                act(E12[:, :], D12[:, :], AF.Exp, ["D12"], ["E12"])
                tt(mb[:, :], Mxb[:, :], sp4[:, :], ALU.subtract, ["Mxb", "sp4", "D12"], ["mb"])

                def ev(bank, bres, n0, n):
                    for hh in range(2):
                        h = n0 // 256 + hh
                        ts(vg[:L, h, 0:256], bank[:L, hh * 256:(hh + 1) * 256], E12[:L, h:h + 1], None, ALU.mult, None,
                           [bres, "E12"], ["vg"])
                proj_tm(slab, 0, 1024, 8, xT, "xT", L, ev)
                vcopy(vg[:L, :, 256], E12[:L, 0:4], ["E12"], ["vg"])
            stage(wslab(W, l, 8, 2048, 3080), 8, 1032, f"w_in_b{l}", s2)

            def s3(slab):
                def ev(view, bres, g0, n):
                    act(sgmo[:, g0:g0 + n, :L], view, AF.Sigmoid, [bres], ["sgmo"])
                proj_fm(slab, 0, 8, 8, xT, "xT", L, ev)
                for h in range(4):
                    AT = ATb[h % 2]; ares = f"AT{h % 2}"
                    Cb = Csb[h % 2]; cres = f"Csb{h % 2}"
                    for c in range(2):
                        mm(PA1[:L, 0:L], kT[:, 2 * h + c, :L], qT[:, 2 * h + c, :L], c == 0, c == 1, ["bk", "bq"], ["ps5"])
                    tt(AT[:L, :L], PA1[:L, 0:L], maskU[:L, :L], ALU.mult, ["ps5", "maskU"], [ares])
                    ts(Cb[:, :, :], Caug[:, h, :, :], E12[:, 4 + h:5 + h], None, ALU.mult, None, ["Caug", "E12"], [cres])
                    mm(PY[:L, 0:257], qT[:, 2 * h, :L], Cb[:, 0, :], True, False, ["bq", cres], ["ps4"])
                    mm(PY[:L, 0:257], qT[:, 2 * h + 1, :L], Cb[:, 1, :], False, False, ["bq", cres], ["ps4"])
                    mm(PY[:L, 0:257], AT[:L, :L], vg[:L, h, :], False, True, [ares, "vg"], ["ps4"])
                    ts(den[:L, 0:1], PY[:L, 256:257], -1.0, None, ALU.mult, None, ["ps4"], ["den"])
                    tt(den[:L, 0:1], den[:L, 0:1], PY[:L, 256:257], ALU.max, ["den", "ps4"], ["den"])
                    tt(den[:L, 0:1], den[:L, 0:1], E12[:L, 8 + h:9 + h], ALU.max, ["den", "E12"], ["den"])
                    recip(den[:L, 1:2], den[:L, 0:1], ["den"], ["den"])
                    ts(hbuf[:L, h, :], PY[:L, 0:256], den[:L, 1:2], None, ALU.mult, None, ["ps4", "den"], ["fA"])
                    for c in range(2):
                        bank, bres = (PX0, "ps6") if c == 0 else (PX1, "ps7")
                        mm(bank[:, 0:257], bkt[:L, h * 256 + c * 128:h * 256 + (c + 1) * 128], vg[:L, h, :], True, True,
                           ["bkt", "vg"], [bres])
                        stt(Caug[:, h, c, :], Caug[:, h, c, :], E12[:, 4 + h:5 + h], bank[:, 0:257], ALU.mult, ALU.add,
                            ["Caug", "E12", bres], ["Caug"])
                for h in range(4):
                    p.op("dve", lambda e, h=h: e.bn_stats(out=bst[:L, h, :], in_=hbuf[:L, h, :]), ["fA"], ["bst"])
                    p.op("dve", lambda e, h=h: e.bn_aggr(out=bmv[:L, h, :], in_=bst[:L, h:h + 1, :]), ["bst"], ["bmv"])
                act(rstd[:L, :], bmv[:L, :, 1], AF.Ln, ["bmv"], ["rstd"], bias=LN_EPS)
                act(rstd[:L, :], rstd[:L, :], AF.Exp, ["rstd"], ["rstd"], scale=-0.5)
                for h in range(4):
                    ts(hbuf[:L, h, :], hbuf[:L, h, :], bmv[:L, h, 0:1], rstd[:L, h:h + 1], ALU.subtract, ALU.mult,
                       ["fA", "bmv", "rstd"], ["fA"])
                for g0 in (0, 4):
                    for j in range(4):
                        tr(PTR[:, j * 128:j * 128 + L], fA[:L, (g0 + j) * 128:(g0 + j + 1) * 128], identf[:L, :L],
                           ["fA", "identf"], ["ps2"])
                    view = PTR[:].rearrange("p (j l) -> p j l", l=128)[:, :, :L]
                    tt(tmpf[:, :, :L], view, mnc[:, g0:g0 + 4, None].to_broadcast([128, 4, L]), ALU.mult, ["ps2", "mnc"], ["tmpf"])
                    tt(hnT[:, g0:g0 + 4, :L], tmpf[:, :, :L], sgmo[:, g0:g0 + 4, :L], ALU.mult, ["tmpf", "sgmo"], ["bh"])
            stage(wslab(W, l, 8, 3080, 4104), 8, 1024, f"w_in_b{l}", s3)

            slot_own = ti % 5

            def s4(slab):
                def ev(view, bres, g0, n):
                    act(aqT[:, :, :L], view, AF.Copy, [bres], ["aqT"], scale=0.125)
                proj_fm(slab, 0, 4, 8, xT, "xT", L, ev)

                def ev2(view, bres, g0, n):
                    acopy(akT[slot_own][:, :, :L], view, [bres], [f"akT{slot_own}"])
                proj_fm(slab, 512, 4, 8, xT, "xT", L, ev2)
                if kdst is not None:
                    def ev3(bank, bres, n0, n):
                        acopy(kst[:L, :], bank[:L, 0:512], [bres], ["kst"])
                    proj_tm(slab, 512, 512, 8, xT, "xT", L, ev3)
                    dma("act", "st", kdst, kst[:L, :], ["kst"], [])
            stage(wslab(W, l, 8, 4104, 5128), 8, 1024, f"w_in_b{l}", s4)

            def s5(slab):
                def ev(bank, bres, n0, n):
                    acopy(avr[slot_own][:L, :, 0:64], bank[:L, 0:512].rearrange("p (h d) -> p h d", h=8), [bres], [f"avr{slot_own}"])
                    if vdst is not None:
                        vcopy(vst[:L, :], bank[:L, 0:512], [bres], ["vst"])
                proj_tm(slab, 0, 512, 8, xT, "xT", L, ev)
                if vdst is not None:
                    dma("act", "st", vdst, vst[:L, :], ["vst"], [])
                tiles = [(ti - dlt, dlt) for dlt in (4, 3, 2, 1, 0) if ti - dlt >= 0]
                for h in range(8):
                    c = h // 2
                    pb = (h % 2) * 64
                    PT = PTb[h % 2]; pres = f"PT{h % 2}"
                    obank, obres = (PX0, "ps6") if h < 4 else (PX1, "ps7")
                    for (j, dlt) in tiles:
                        Lk = L if dlt == 0 else 128
                        sl_ = j % 5
                        if dlt == 0:
                            dst = PA1[:Lk, 0:L]; dres = "ps5"
                        else:
                            dst = PY[:Lk, (4 - dlt) * 128:(4 - dlt) * 128 + L]; dres = "ps4"
                        mm(dst, akT[sl_][pb:pb + 64, c, :Lk], aqT[pb:pb + 64, c, :L], True, True, [f"akT{sl_}", "aqT"], [dres])
                    for ii, (j, dlt) in enumerate(tiles):
                        Lk = L if dlt == 0 else 128
                        if dlt == 0:
                            src = PA1[:Lk, 0:L]; sres = "ps5"
                        else:
                            src = PY[:Lk, (4 - dlt) * 128:(4 - dlt) * 128 + L]; sres = "ps4"
                        if dlt in (2, 3):
                            act(PT[:Lk, 4 - dlt, :L], src, AF.Exp, [sres], [pres])
                        else:
                            tmp = atmp[ii % 2]; tres = f"atmp{ii % 2}"
                            if dlt == 0:
                                tt(tmp[:Lk, :L], src, BT0[:Lk, h, :L], ALU.add, [sres, "BT0"], [tres])
                            elif dlt == 1:
                                tt(tmp[:Lk, :L], src, BT1[:Lk, h, :L], ALU.add, [sres, "BT1"], [tres])
                            else:
                                tt(tmp[:Lk, :L], src, mask4[:Lk, :L], ALU.add, [sres, "mask4"], [tres])
                            act(PT[:Lk, 4 - dlt, :L], tmp[:Lk, :L], AF.Exp, [tres], [pres])
                    for ii, (j, dlt) in enumerate(tiles):
                        Lk = L if dlt == 0 else 128
                        sl_ = j % 5
                        mm(obank[:L, (h % 4) * 65:(h % 4) * 65 + 65], PT[:Lk, 4 - dlt, :L], avr[sl_][:Lk, h, :],
                           ii == 0, ii == len(tiles) - 1, [pres, f"avr{sl_}"], [obres])
                for half, (obank, obres) in enumerate(((PX0, "ps6"), (PX1, "ps7"))):
                    ov = obank[:L, 0:260].rearrange("p (h d) -> p h d", h=4)
                    recip(rs8[:L, half * 4:half * 4 + 4], ov[:, :, 64], [obres], ["rs8"])
                    tt(oa[:L, half * 4:half * 4 + 4, :], ov[:, :, 0:64],
                       rs8[:L, half * 4:half * 4 + 4, None].to_broadcast([L, 4, 64]), ALU.mult, [obres, "rs8"], ["oa"])
                transpose_to(oaT, "oaT", oa[:].rearrange("p h d -> p (h d)"), "oa", L, 512)
            stage(wslab(W, l, 8, 5128, 5640), 8, 512, f"w_in_b{l}", s5)

            def s6(slab):
                def ev(view, bres, g0, n):
                    act(sggm[:, g0:g0 + n, :L], view, AF.Sigmoid, [bres], ["sggm"])
                proj_fm(slab, 0, 8, 8, xT, "xT", L, ev)
            stage(wslab(W, l, 8, 5640, 6664), 8, 1024, f"w_in_b{l}", s6)

            def s7(slab):
                def ev(view, bres, g0, n):
                    act(sgga[:, g0:g0 + n, :L], view, AF.Sigmoid, [bres], ["sgga"])
                proj_fm(slab, 0, 8, 8, xT, "xT", L, ev)
            stage(wslab(W, l, 8, 6664, 7688), 8, 1024, f"w_in_b{l}", s7)

            def s8(slab):
                def ev(view, bres, g0, n):
                    tt(m1[:, g0:g0 + n, :L], view, sggm[:, g0:g0 + n, :L], ALU.mult, [bres, "sggm"], ["fB"])
                proj_fm(slab, 0, 8, 8, hnT, "bh", L, ev)
            stage(wslab(wbm_b, l, 8, 0, 1024), 8, 1024, f"wbm_b{l}", s8)

            def s9(slab):
                def ev(view, bres, g0, n):
                    tt(tmpf[:, 0:n, :L], view, sgga[:, g0:g0 + n, :L], ALU.mult, [bres, "sgga"], ["tmpf"])
                    tt(mixT[:, g0:g0 + n, :L], tmpf[:, 0:n, :L], m1[:, g0:g0 + n, :L], ALU.add, ["tmpf", "fB"], ["mixT"])
                proj_fm(slab, 0, 8, 4, oaT, "oaT", L, ev)
            stage(wslab(wba_b, l, 4, 0, 1024), 4, 1024, f"wba_b{l}", s9)

            def s10(slab):
                def ev(bank, bres, n0, n):
                    stt(fA[:L, n0:n0 + n], xb[:L, n0:n0 + n], ALPHA, bank[:L, 0:n], ALU.mult, ALU.add, ["xb", bres], ["fA"])
                proj_tm(slab, 0, 1024, 8, mixT, "mixT", L, ev)
                layer_norm(fA, "fA", x1, "x1", lng1, lnb1, L)
                dump(f"x1_{l}_{ti}_{L}", x1[:L, :], "x1", [L, D])
                transpose_to(x1T, "bk", x1, "x1", L, 1024)
                vcopy(bq[:L, :], x1[:L, :], ["x1"], ["bq"])
            stage(wslab(wo_b, l, 8, 0, 1024), 8, 1024, f"wo_b{l}", s10)

            def s11(slab):
                def ev(view, bres, g0, n):
                    acopy(qrT[:, g0:g0 + n, :L], view, [bres], ["qrT"])
                proj_fm(slab, 0, 8, 8, x1T, "bk", L, ev)
            stage(wslab(wq_b, l, 8, 0, 1024), 8, 1024, f"wq_b{l}", s11)

            def s12(slab):
                def ev(view, bres, g0, n):
                    acopy(qrT[:, 8 + g0:8 + g0 + n, :L], view, [bres], ["qrT"])
                proj_fm(slab, 0, 8, 8, x1T, "bk", L, ev)
                peer(l, L)
                layer_norm(fA, "fA", fB, "fB", lng2, lnb2, L)
                transpose_to(x2T, "bh", fB, "fB", L, 1024)
                transpose_to(peT, "peT", peb, "peb", L, 256)
            stage(wslab(wq_b, l, 8, 1024, 2048), 8, 1024, f"wq_b{l}", s12)

            def s13(slab):
                def ev(bank, bres, n0, n):
                    pass
                proj_tm(slab, 0, 1024, 2, peT, "peT", L, ev, banks=[(PS[0], "ps0"), (PS[1], "ps1")])
            stage(wslab(plp_b, l, 2, 0, 1024), 2, 1024, f"plp_b{l}", s13)

            def s14(slab):
                def ev(bank, bres, n0, n):
                    act(fA[:L, n0:n0 + n], bank[:L, 0:n], AF.Sigmoid, [bres], ["fA"])
                proj_tm(slab, 0, 1024, 8, x2T, "bh", L, ev, banks=[(PX0, "ps6"), (PX1, "ps7")])
                for half in range(2):
                    n0 = half * 512
                    tt(fA[:L, n0:n0 + 512], fA[:L, n0:n0 + 512], PS[half][:L, 0:512], ALU.mult, ["fA", f"ps{half}"], ["fA"])
                tt(fA[:L, :], fA[:L, :], fB[:L, :], ALU.add, ["fA", "fB"], ["fA"])
                dma("act", "st", xdst, fA[:L, :], ["fA"], [xdres] if xdres else [])
            stage(wslab(plg_b, l, 8, 0, 1024), 8, 1024, f"plg_b{l}", s14)

        def peer(l, L):
            sc3 = sc[:].rearrange("p (g n) -> p g n", g=16)
            banks = [(PS[0], "ps0"), (PS[1], "ps1"), (PY, "ps4"), (PA1, "ps5")]
            memset(dots[:L, :], 0.0, ["dots"], eng="dve")
            for b4 in range(4):
                bank, bres = banks[b4]
                for j in range(4):
                    g = b4 * 4 + j
                    mm(bank[:L, j * 128:(j + 1) * 128], qrT[:, g, :L], skT[:, g, :], True, True, ["qrT", "skT"], [bres])
                acopy(sc[:L, b4 * 512:(b4 + 1) * 512], bank[:L, :], [bres], ["sc"])
            for g in range(16):
                s2 = sc2[g % 2]; s2r = f"sc2_{g % 2}"
                p.op("dve", lambda e, g=g: e.max(out=top[:L, g, 0:8], in_=sc3[:L, g, :]), ["sc"], ["top"])
                p.op("dve", lambda e, g=g: e.max_index(out=idxu[:L, g, 0:8], in_max=top[:L, g, 0:8], in_values=sc3[:L, g, :]),
                     ["sc", "top"], ["idxu"])
                p.op("dve", lambda e, g=g, s2=s2: e.match_replace(out=s2[:L, :], in_to_replace=top[:L, g, 0:8],
                                                                 in_values=sc3[:L, g, :], imm_value=-1e30), ["sc", "top"], [s2r])
                p.op("dve", lambda e, g=g, s2=s2: e.max(out=top[:L, g, 8:16], in_=s2[:L, :]), [s2r], ["top"])
                p.op("dve", lambda e, g=g, s2=s2: e.max_index(out=idxu[:L, g, 8:16], in_max=top[:L, g, 8:16], in_values=s2[:L, :]),
                     [s2r, "top"], ["idxu"])
            vcopy(sif[:L], idxu[:L], ["idxu"], ["sif"])
            topv = top[:].rearrange("p (h t) k -> p h t k", t=2)
            sifv = sif[:].rearrange("p (h t) k -> p h t k", t=2)
            cand = sc[:].rearrange("p (h a b) -> p h a b", h=8, a=16)
            cand3 = sc[:].rearrange("p (h n) -> p h n", h=8)
            tt(cand[:L], topv[:L, :, 0, :, None].to_broadcast([L, 8, 16, 16]), topv[:L, :, 1, None, :].to_broadcast([L, 8, 16, 16]),
               ALU.add, ["top"], ["sc"])
            for h in range(8):
                c2 = cand2[h % 2]; c2r = f"cand2_{h % 2}"
                p.op("dve", lambda e, h=h: e.max(out=fv[:L, h, 0:8], in_=cand3[:L, h, :]), ["sc"], ["fv"])
                p.op("dve", lambda e, h=h: e.max_index(out=fiu[:L, h, 0:8], in_max=fv[:L, h, 0:8], in_values=cand3[:L, h, :]),
                     ["sc", "fv"], ["fiu"])
                p.op("dve", lambda e, h=h, c2=c2: e.match_replace(out=c2[:L, :], in_to_replace=fv[:L, h, 0:8],
                                                                 in_values=cand3[:L, h, :], imm_value=-1e30), ["sc", "fv"], [c2r])
                p.op("dve", lambda e, h=h, c2=c2: e.max(out=fv[:L, h, 8:16], in_=c2[:L, :]), [c2r], ["fv"])
                p.op("dve", lambda e, h=h, c2=c2: e.max_index(out=fiu[:L, h, 8:16], in_max=fv[:L, h, 8:16], in_values=c2[:L, :]),
                     [c2r, "fv"], ["fiu"])
            tsc(fau[:L], fiu[:L], 4, ALU.logical_shift_right, ["fiu"], ["fau"])
            tsc(fbu[:L], fiu[:L], 15, ALU.bitwise_and, ["fiu"], ["fbu"])
            vcopy(faf[:L], fau[:L], ["fau"], ["faf"])
            vcopy(fbf[:L], fbu[:L], ["fbu"], ["fbf"])
            for (ff, fres, t_, sel, sres) in ((faf, "faf", 0, sel0, "sel0"), (fbf, "fbf", 1, sel1, "sel1")):
                tt(cand[:L], ff[:L, :, :, None].to_broadcast([L, 8, 16, 16]), io16[:L, None, None, :].to_broadcast([L, 8, 16, 16]),
                   ALU.is_equal, [fres, "io16", "fiu"], ["sc"])
                tt(cand[:L], cand[:L], sifv[:L, :, t_, None, :].to_broadcast([L, 8, 16, 16]), ALU.mult, ["sc", "sif"], ["sc"])
                red(sel[:L], cand[:L], ALU.add, ["sc"], [sres])
            stt(eidxf[:L, :], sel0[:L].rearrange("p h k -> p (h k)"), 128.0, sel1[:L].rearrange("p h k -> p (h k)"),
                ALU.mult, ALU.add, ["sel0", "sel1"], ["eidxf"])
            ts(eidxf[:L, :], eidxf[:L, :], 0.0, 16383.0, ALU.max, ALU.min, ["eidxf"], ["eidxf"])
            vcopy(eidxi[:L, :], eidxf[:L, :], ["eidxf"], ["eidxi"])
            tt(gw[:L], fv[:L], fv[:L, :, 0:1].to_broadcast([L, 8, 16]), ALU.subtract, ["fv"], ["gw"])
            act(gw[:L], gw[:L], AF.Exp, ["gw"], ["gw"])
            red(gsum[:L], gw[:L], ALU.add, ["gw"], ["gsum"])
            recip(gsum[:L], gsum[:L], ["gsum"], ["gsum"])
            tt(gw[:L], gw[:L], gsum[:L, :, None].to_broadcast([L, 8, 16]), ALU.mult, ["gw", "gsum"], ["gw"])
            nch = 128 // GS
            gctr = peer.gctr
            for ci in range(nch):
                k = gctr[0] % NGB
                gctr[0] += 1
                gbuf = gbs[k]
                for s in range(GS):
                    j = ci * GS + s
                    p.dma("pool", f"gq{k}", lambda e, gbuf=gbuf, s=s, j=j: e.indirect_dma_start(
                        out=gbuf[:L, s, :], out_offset=None, in_=ub[l],
                        in_offset=bass.IndirectOffsetOnAxis(ap=eidxi[:L, j:j + 1], axis=0)),
                        wnames[f"ub{l}"] + ["eidxi", f"gbdone{k}"], [f"gb{k}_{s}"], group=True, nowaw=True)
                for s in range(GS):
                    j = ci * GS + s
                    stt(bkt[:L, :], gbuf[:L, s, :], 1.0, bq[:L, :], ALU.mult, ALU.mult, [f"gb{k}_{s}", "bq"],
                        ["bkt", "dots"] + ([f"gbdone{k}"] if s == GS - 1 else []), accum=dots[:L, j:j + 1])
            tt(ge1[:L], dots[:L], dots[:L], ALU.mult, ["dots"], ["ge1"])
            ts(ge1[:L], ge1[:L], 0.044715, 1.0, ALU.mult, ALU.add, ["ge1"], ["ge1"])
            tt(ge1[:L], ge1[:L], dots[:L], ALU.mult, ["ge1", "dots"], ["ge1"])
            act(ge2[:L], ge1[:L], AF.Sigmoid, ["ge1"], ["ge2"], scale=1.5957691216057308)
            tt(ge2[:L], ge2[:L], dots[:L], ALU.mult, ["ge2", "dots"], ["ge2"])
            tt(wgt[:L], ge2[:L], gw[:L].rearrange("p h k -> p (h k)"), ALU.mult, ["ge2", "gw"], ["wgt"])
            for ci in range(nch):
                k = gctr[0] % NGB
                gctr[0] += 1
                gbuf = gbs[k]
                dg = dgm[ci % 2]; dres = f"dgm{ci % 2}"
                for s in range(GS):
                    j = ci * GS + s
                    p.dma("pool", f"gq{k}", lambda e, gbuf=gbuf, s=s, j=j: e.indirect_dma_start(
                        out=gbuf[:L, s, :], out_offset=None, in_=vb[l],
                        in_offset=bass.IndirectOffsetOnAxis(ap=eidxi[:L, j:j + 1], axis=0)),
                        wnames[f"vb{l}"] + ["eidxi", f"gbdone{k}"], [f"gb{k}_{s}"], group=True, nowaw=True)
                tt(dg[:L, :, :L], identb[:L, None, :L].to_broadcast([L, GS, L]),
                   wgt[:L, ci * GS:(ci + 1) * GS, None].to_broadcast([L, GS, L]), ALU.mult, ["identb", "wgt"], [dres])
                for s in range(GS):
                    j = ci * GS + s
                    for n in range(2):
                        bank, bres = (PX0, "ps6") if n == 0 else (PX1, "ps7")
                        mm(bank[:L, 0:512], dg[:L, s, :L], gbuf[:L, s, n * 512:(n + 1) * 512], j == 0, j == 127,
                           [dres, f"gb{k}_{s}"], [bres] + ([f"gbdone{k}"] if (s == GS - 1 and n == 1) else []))
            for n in range(2):
                bank, bres = (PX0, "ps6") if n == 0 else (PX1, "ps7")
                stt(fA[:L, n * 512:(n + 1) * 512], x1[:L, n * 512:(n + 1) * 512], ALPHA, bank[:L, 0:512], ALU.mult, ALU.add,
                    ["x1", bres], ["fA"])
        peer.gctr = [0]

        for l in range(DEPTH):
            stage(None, 0, 0, None, lambda slab, l=l: layer_setup(l))
            for s in range(NP):
                stage(None, 0, 0, None, lambda slab: seq_start_prompt())
                for t in range(NT):
                    xsrc = (xp if l == 0 else xmp)[s, t * 128:(t + 1) * 128, :]
                    xres = None if l == 0 else f"xmp_{s}_{t}"
                    xdst = (xmp if l == 0 else yp)[s, t * 128:(t + 1) * 128, :]
                    xdres = f"xmp_{s}_{t}" if l == 0 else None
                    kd = vd = None
                    if t >= NT - 4:
                        r0 = (t - (NT - 4)) * 128
                        kd = kp_o[l, s, r0:r0 + 128, :]
                        vd = vp_o[l, s, r0:r0 + 128, :]
                    tile(l, 128, t, xsrc, xres, xdst, xdres, pp[l, s, t * 128:(t + 1) * 128, :], kd, vd)
                stage(None, 0, 0, None, lambda slab, l=l, s=s: seq_end(l, Cp_o[l, s], np_o[l, s], mp_o[l, s:s + 1, :]))
            for b in range(NS):
                stage(None, 0, 0, None, lambda slab, l=l, b=b: seq_start_sample(l, b))
                xsrc = (xs if l == 0 else xms)[b]
                xres = None if l == 0 else f"xms_{b}"
                xdst = (xms if l == 0 else ys)[b]
                xdres = f"xms_{b}" if l == 0 else None
                tile(l, 64, 8, xsrc, xres, xdst, xdres, psm[l, b], ks_o[l, b], vs_o[l, b])
                stage(None, 0, 0, None, lambda slab, l=l, b=b: seq_end(l, Cs_o[l, b], ns_o[l, b], ms_o[l, b:b + 1, :]))

        nxt_i = 0
        pending = {}

        def prefetch_upto(i):
            nonlocal nxt_i
            while nxt_i <= i and nxt_i < len(stages):
                pending[nxt_i] = load_slab(stages[nxt_i])
                nxt_i += 1

        if limit is not None:
            stages = stages[:limit]
        for i, st in enumerate(stages):
            prefetch_upto(i)
            j = i + 1
            while j < len(stages) and stages[j][0] is None:
                j += 1
            if st[0] is not None:
                prefetch_upto(j)
            st[4](pending.pop(i))

        p.final_wait("sp")
        p.emit()
    return nc, dbg_outs


def make_in_maps(inputs, n_cores, NP, NS):
    f = lambda a: np.ascontiguousarray(np.asarray(a, dtype=np.float32))
    maps = []
    for c in range(n_cores):
        ps_ = slice(c * NP, (c + 1) * NP)
        ss = slice(c * NS, (c + 1) * NS)
        m = {
            "xp": f(inputs["x_prompt"][ps_]),
            "xs": f(inputs["x_sample"][ss]),
            "ck": f(np.asarray(inputs["cache_attn_k"])[:, ss].reshape(DEPTH, NS, 512, 512)),
            "cv": f(np.asarray(inputs["cache_attn_v"])[:, ss].reshape(DEPTH, NS, 512, 512)),
            "sC": f(np.asarray(inputs["state_mlstm_C"])[:, ss]),
            "sn": f(np.asarray(inputs["state_mlstm_n"])[:, ss]),
            "sm": f(np.asarray(inputs["state_mlstm_m"])[:, ss]),
            "pp": f(np.asarray(inputs["p_prompt"])[:, ps_]),
            "psm": f(np.asarray(inputs["p_sample"])[:, ss]),
            "w_in": f(inputs["w_in"]),
            "ifb": f(np.asarray(inputs["mlstm_if_bias"]).reshape(DEPTH, 8)),
            "mnw": f(np.asarray(inputs["mlstm_norm_w"]).reshape(DEPTH, 1024)),
            "rb": f(inputs["attn_rel_bias"]),
            "wbm": f(inputs["w_branch_m"]), "wba": f(inputs["w_branch_a"]), "wo": f(inputs["w_out"]),
            "ln1g": f(inputs["ln1_g"]), "ln1b": f(inputs["ln1_b"]), "ln2g": f(inputs["ln2_g"]), "ln2b": f(inputs["ln2_b"]),
            "wq": f(inputs["peer_wq"]),
            "sk": f(np.asarray(inputs["peer_subkeys"]).reshape(DEPTH, 16, 128, 128)),
            "pu": f(inputs["peer_u"]), "pv": f(inputs["peer_v"]),
            "plp": f(inputs["ple_proj"]), "plg": f(inputs["ple_gate"]),
        }
        maps.append(m)
    return maps


def assemble(results, NP, NS):
    cat = lambda k, ax: np.concatenate([np.asarray(r[k]) for r in results], axis=ax)
    yp = cat("yp", 0); ys = cat("ys", 0)
    kp = cat("kp", 1); vp = cat("vp", 1)
    kp = kp.reshape(kp.shape[0], kp.shape[1], kp.shape[2], 8, 64); vp = vp.reshape(kp.shape)
    Cp = cat("Cp", 1); npp = cat("np", 1); mp = cat("mp", 1)
    ks = cat("ks", 1); vs = cat("vs", 1)
    ks = ks.reshape(ks.shape[0], ks.shape[1], ks.shape[2], 8, 64); vs = vs.reshape(ks.shape)
    Cs = cat("Cs", 1); ns = cat("ns", 1); ms = cat("ms", 1)
    return tuple(np.ascontiguousarray(a, dtype=np.float32) for a in (yp, ys, kp, vp, Cp, npp, mp, ks, vs, Cs, ns, ms))


def kernel(**inputs):
    n_cores = 8
    B = inputs["x_prompt"].shape[0]
    SEQ = inputs["x_prompt"].shape[1]
    BS = inputs["x_sample"].shape[0]
    NP = B // n_cores
    NS = BS // n_cores
    nc, _ = build(NP, SEQ, NS)
    maps = make_in_maps(inputs, n_cores, NP, NS)
    res = run_bass_kernel_spmd(nc, maps, core_ids=list(range(n_cores)))
    return assemble(res.results, NP, NS)
```

```python
import contextlib
import numpy as np
import concourse.bass as bass
import concourse.mybir as mybir
from concourse.bass_utils import run_bass_kernel_spmd

F32 = mybir.dt.float32
BF16 = mybir.dt.bfloat16
I32 = mybir.dt.int32
U32 = mybir.dt.uint32
AF = mybir.ActivationFunctionType
ALU = mybir.AluOpType
AX = mybir.AxisListType

D = 1024
INC = 7688
DEPTH = 2
ALPHA = (2 * DEPTH) ** 0.25
LN_EPS = 1e-5
NEG = -30000.0
ENGS = ("pe", "dve", "act", "pool", "sp")
GS = 4
NGB = 3
NSLAB = 3
SLABW = 1032


class Prog:
    def __init__(self, nc, es, same_engine_sync=True):
        self.nc = nc
        self.es = es
        self.q = {e: [] for e in ENGS}
        self.cnt = {e: 0 for e in ENGS}
        self.semh = {}
        for e in ENGS:
            self.semh[e] = es.enter_context(nc.semaphore("s_" + e))
        self.known = {e: {} for e in ENGS}
        self.lastw = {}
        self.readers = {}
        self.dcount = {}
        self.dcls = {}
        self.groupsem = set()
        self.same = same_engine_sync

    def sb(self, name, shape, dt):
        return self.es.enter_context(self.nc.sbuf_tensor(name, list(shape), dt))

    def ps(self, name, shape, dt=F32):
        return self.es.enter_context(self.nc.psum_tensor(name, list(shape), dt))

    def _deps(self, eng, reads, writes, nowaw=False):
        need = {}

        def add(ev):
            if ev is None:
                return
            k, v = ev
            if k in self.groupsem:
                v = self.dcount[k]
            if need.get(k, 0) < v:
                need[k] = v

        for r in reads:
            add(self.lastw.get(r))
        for w in writes:
            if not nowaw:
                add(self.lastw.get(w))
            for k, v in self.readers.get(w, {}).items():
                add((k, v))
        waits = []
        for k, v in need.items():
            if k == eng and (eng == "pe" or not self.same):
                continue
            if self.known[eng].get(k, 0) < v:
                self.known[eng][k] = v
                waits.append((k, v))
        return waits

    def _record(self, ev, reads, writes):
        k, v = ev
        for w in writes:
            self.lastw[w] = ev
            self.readers[w] = {}
        for r in reads:
            d = self.readers.setdefault(r, {})
            if d.get(k, 0) < v:
                d[k] = v

    @staticmethod
    def _l(x):
        return [x] if isinstance(x, str) else list(x)

    def op(self, eng, fn, reads=(), writes=()):
        reads, writes = self._l(reads), self._l(writes)
        pr = [r for r in reads if r.startswith("ps")]
        if pr:
            writes = writes + [r for r in pr if r not in writes]
            reads = [r for r in reads if not r.startswith("ps")]
        waits = self._deps(eng, reads, writes)
        self.cnt[eng] += 1
        val = self.cnt[eng]
        sem = self.semh[eng]
        semh = self.semh

        def thunk(e):
            for k, v in waits:
                e.wait_ge(semh[k], v)
            fn(e).then_inc(sem, 1)

        self.q[eng].append(thunk)
        self._record((eng, val), reads, writes)

    def dma(self, eng, cls, fn, reads=(), writes=(), ring=1, group=False, nowaw=False):
        reads, writes = self._l(reads), self._l(writes)
        st = self.dcls.setdefault(cls, {"i": 0, "keys": []})
        if len(st["keys"]) < ring:
            key = "%s#%d" % (cls, len(st["keys"]))
            st["keys"].append(key)
            self.semh[key] = self.es.enter_context(self.nc.semaphore("d_" + key.replace("#", "_")))
            self.dcount[key] = 0
            if group:
                self.groupsem.add(key)
        key = st["keys"][st["i"] % ring]
        st["i"] += 1
        waits = self._deps(eng, reads, writes, nowaw=nowaw)
        prev = self.dcount[key]
        if not group and prev > 0 and self.known[eng].get(key, 0) < prev:
            self.known[eng][key] = prev
            waits.append((key, prev))
        self.dcount[key] += 16
        val = self.dcount[key]
        sem = self.semh[key]
        semh = self.semh

        def thunk(e):
            for k, v in waits:
                e.wait_ge(semh[k], v)
            fn(e).then_inc(sem, 16)

        self.q[eng].append(thunk)
        self._record((key, val), reads, writes)

    def final_wait(self, eng="sp"):
        waits = []
        for k in self.semh:
            v = self.cnt[k] if k in ENGS else self.dcount[k]
            if k != eng and v > 0:
                waits.append((k, v))
        semh = self.semh

        def thunk(e):
            for k, v in waits:
                e.wait_ge(semh[k], v)

        self.q[eng].append(thunk)

    def emit(self):
        nc = self.nc
        q = self.q
        with nc.Block() as block:

            @block.tensor
            def _(e):
                for t in q["pe"]:
                    t(e)

            @block.vector
            def _(e):
                for t in q["dve"]:
                    t(e)

            @block.scalar
            def _(e):
                for t in q["act"]:
                    t(e)

            @block.gpsimd
            def _(e):
                for t in q["pool"]:
                    t(e)

            @block.sync
            def _(e):
                for t in q["sp"]:
                    t(e)


def build(NP, SEQ, NS, debug=None, limit=None, nocast=False):
    nc = bass.Bass("TRN2", target_bir_lowering=False)
    NT = SEQ // 128
    assert SEQ % 128 == 0 and NT >= 4

    def din(name, shape, dt=F32):
        return nc.dram_tensor(name, list(shape), dt, kind="ExternalInput").ap()

    def dout(name, shape, dt=F32):
        return nc.dram_tensor(name, list(shape), dt, kind="ExternalOutput").ap()

    def dint(name, shape, dt=F32):
        return nc.dram_tensor(name, list(shape), dt, kind="Internal").ap()

    xp = din("xp", [NP, SEQ, D]); xs = din("xs", [NS, 64, D])
    ck = din("ck", [DEPTH, NS, 512, 512]); cv = din("cv", [DEPTH, NS, 512, 512])
    sC = din("sC", [DEPTH, NS, 4, 256, 256]); sn = din("sn", [DEPTH, NS, 4, 256]); sm = din("sm", [DEPTH, NS, 4])
    pp = din("pp", [DEPTH, NP, SEQ, 256]); psm = din("psm", [DEPTH, NS, 64, 256])
    w_in = din("w_in", [DEPTH, D, INC]); ifb = din("ifb", [DEPTH, 8]); mnw = din("mnw", [DEPTH, 1024])
    rb = din("rb", [DEPTH, 8, 257])
    wbm = din("wbm", [DEPTH, 1024, 1024]); wba = din("wba", [DEPTH, 512, 1024]); wo = din("wo", [DEPTH, 1024, 1024])
    ln1g = din("ln1g", [DEPTH, D]); ln1b = din("ln1b", [DEPTH, D]); ln2g = din("ln2g", [DEPTH, D]); ln2b = din("ln2b", [DEPTH, D])
    wq = din("wq", [DEPTH, 1024, 2048]); sk = din("sk", [DEPTH, 16, 128, 128])
    pu = din("pu", [DEPTH, 16384, 1024]); pv = din("pv", [DEPTH, 16384, 1024])
    plp = din("plp", [DEPTH, 256, 1024]); plg = din("plg", [DEPTH, 1024, 1024])

    yp = dout("yp", [NP, SEQ, D]); ys = dout("ys", [NS, 64, D])
    kp_o = dout("kp", [DEPTH, NP, 512, 512]); vp_o = dout("vp", [DEPTH, NP, 512, 512])
    Cp_o = dout("Cp", [DEPTH, NP, 4, 256, 256]); np_o = dout("np", [DEPTH, NP, 4, 256]); mp_o = dout("mp", [DEPTH, NP, 4])
    ks_o = dout("ks", [DEPTH, NS, 64, 512]); vs_o = dout("vs", [DEPTH, NS, 64, 512])
    Cs_o = dout("Cs", [DEPTH, NS, 4, 256, 256]); ns_o = dout("ns", [DEPTH, NS, 4, 256]); ms_o = dout("ms", [DEPTH, NS, 4])

    xmp = dint("xmp", [NP, SEQ, D]); xms = dint("xms", [NS, 64, D])
    w_in_b = dint("w_in_b", [DEPTH, D, INC], BF16)
    wbm_b = dint("wbm_b", [DEPTH, 1024, 1024], BF16); wba_b = dint("wba_b", [DEPTH, 512, 1024], BF16)
    wo_b = dint("wo_b", [DEPTH, 1024, 1024], BF16); wq_b = dint("wq_b", [DEPTH, 1024, 2048], BF16)
    plp_b = dint("plp_b", [DEPTH, 256, 1024], BF16); plg_b = dint("plg_b", [DEPTH, 1024, 1024], BF16)
    ub = [dint(f"ub{l}", [16384, 1024], BF16) for l in range(DEPTH)]
    vb = [dint(f"vb{l}", [16384, 1024], BF16) for l in range(DEPTH)]
    rbx = dint("rbx", [DEPTH, 8, 384])
    dbg_outs = {}

    with contextlib.ExitStack() as es:
        p = Prog(nc, es)

        def mm(out, lhsT, rhs, start, stop, r, w):
            p.op("pe", lambda e: e.matmul(out, lhsT=lhsT, rhs=rhs, start=start, stop=stop), r, w)

        def tr(out, in_, ident, r, w):
            p.op("pe", lambda e: e.transpose(out=out, in_=in_, identity=ident), r, w)

        def act(out, in_, func, r, w, bias=None, scale=None):
            kw = {}
            if bias is not None:
                kw["bias"] = bias
            if scale is not None:
                kw["scale"] = scale
            p.op("act", lambda e: e.activation(out=out, in_=in_, func=func, **kw), r, w)

        def acopy(out, in_, r, w):
            p.op("act", lambda e: e.copy(out=out, in_=in_), r, w)

        def vcopy(out, in_, r, w, eng="dve"):
            p.op(eng, lambda e: e.tensor_copy(out=out, in_=in_), r, w)

        def tt(out, in0, in1, op, r, w, eng="dve"):
            p.op(eng, lambda e: e.tensor_tensor(out=out, in0=in0, in1=in1, op=op), r, w)

        def ts(out, in0, s1, s2, op0, op1, r, w, eng="dve"):
            if s2 is None:
                p.op(eng, lambda e: e.tensor_scalar(out=out, in0=in0, scalar1=s1, scalar2=None, op0=op0), r, w)
            else:
                p.op(eng, lambda e: e.tensor_scalar(out=out, in0=in0, scalar1=s1, scalar2=s2, op0=op0, op1=op1), r, w)

        def stt(out, in0, scalar, in1, op0, op1, r, w, accum=None, eng="dve"):
            if accum is None:
                p.op(eng, lambda e: e.scalar_tensor_tensor(out=out, in0=in0, scalar=scalar, in1=in1, op0=op0, op1=op1), r, w)
            else:
                p.op(eng, lambda e: e.scalar_tensor_tensor(out=out, in0=in0, scalar=scalar, in1=in1, op0=op0, op1=op1,
                                                          accum_out=accum), r, w)

        def tsc(out, in_, scalar, op, r, w):
            p.op("dve", lambda e: e.tensor_single_scalar(out=out, in_=in_, scalar=scalar, op=op), r, w)

        def red(out, in_, op, r, w):
            p.op("dve", lambda e: e.tensor_reduce(out=out, in_=in_, axis=AX.X, op=op), r, w)

        def recip(out, in_, r, w):
            p.op("dve", lambda e: e.reciprocal(out=out, in_=in_), r, w)

        def memset(ap, val, w, eng="pool", r=()):
            p.op(eng, lambda e: e.memset(ap, val), r, w)

        RING = {"setup": 8, "st": 8, "castw": 8, "castt0": 4, "castt1": 4}

        def dma(eng, semkey, out, in_, r, w):
            p.dma(eng, semkey, lambda e: e.dma_start(out=out, in_=in_), r, w, ring=RING.get(semkey, 1))

        def dump(name, ap, res, shape):
            if debug is None or name not in debug or name in dbg_outs:
                return
            o = dout("dbg_" + name, shape)
            dbg_outs[name] = shape
            dma("act", "st", o, ap, [res], [])

        slabs = [p.sb(f"wsl{i}", [128, 8, SLABW], BF16) for i in range(NSLAB)]
        gbs = [p.sb(f"gb{i}", [128, GS, 1024], BF16) for i in range(NGB)]
        lng1 = p.sb("lng1", [128, D], F32); lnb1 = p.sb("lnb1", [128, D], F32)
        lng2 = p.sb("lng2", [128, D], F32); lnb2 = p.sb("lnb2", [128, D], F32)
        xb = p.sb("xb", [128, D], F32)
        fA = p.sb("fA", [128, D], F32)
        fB = p.sb("fB", [128, D], F32)
        x1 = p.sb("x1", [128, D], F32)
        sc = p.sb("sc", [128, 2048], F32)
        sc2 = [p.sb(f"sc2_{i}", [128, 128], F32) for i in range(2)]
        cand2 = [p.sb(f"cand2_{i}", [128, 256], F32) for i in range(2)]
        top = p.sb("top", [128, 16, 16], F32)
        idxu = p.sb("idxu", [128, 16, 16], U32)
        sif = p.sb("sif", [128, 16, 16], F32)
        fv = p.sb("fv", [128, 8, 16], F32)
        fiu = p.sb("fiu", [128, 8, 16], U32)
        fau = p.sb("fau", [128, 8, 16], U32)
        fbu = p.sb("fbu", [128, 8, 16], U32)
        faf = p.sb("faf", [128, 8, 16], F32)
        fbf = p.sb("fbf", [128, 8, 16], F32)
        sel0 = p.sb("sel0", [128, 8, 16], F32)
        sel1 = p.sb("sel1", [128, 8, 16], F32)
        eidxf = p.sb("eidxf", [128, 128], F32)
        eidxi = p.sb("eidxi", [128, 128], I32)
        gw = p.sb("gw", [128, 8, 16], F32)
        gsum = p.sb("gsum", [128, 8], F32)
        dots = p.sb("dots", [128, 128], F32)
        ge1 = p.sb("ge1", [128, 128], F32)
        ge2 = p.sb("ge2", [128, 128], F32)
        wgt = p.sb("wgt", [128, 128], F32)
        xT = p.sb("xT", [128, 8, 128], BF16)
        bq = p.sb("bq", [128, 1024], BF16)
        bk = p.sb("bk", [128, 1024], BF16)
        bkt = p.sb("bkt", [128, 1024], BF16)
        vg = p.sb("vg", [128, 4, 257], BF16)
        sgmo = p.sb("sgmo", [128, 8, 128], BF16)
        sggm = p.sb("sggm", [128, 8, 128], BF16)
        sgga = p.sb("sgga", [128, 8, 128], BF16)
        bh = p.sb("bh", [128, 1024], BF16)
        mixT = p.sb("mixT", [128, 8, 128], BF16)
        qrT = p.sb("qrT", [128, 16, 128], BF16)
        aqT = p.sb("aqT", [128, 4, 128], BF16)
        akT = [p.sb(f"akT{i}", [128, 4, 128], BF16) for i in range(5)]
        avr = [p.sb(f"avr{i}", [128, 8, 65], BF16) for i in range(5)]
        Caug = p.sb("Caug", [128, 4, 2, 257], F32)
        Csb = [p.sb(f"Csb{i}", [128, 2, 257], BF16) for i in range(2)]
        ATb = [p.sb(f"AT{i}", [128, 128], BF16) for i in range(2)]
        BT0 = p.sb("BT0", [128, 8, 128], F32)
        BT1 = p.sb("BT1", [128, 8, 128], F32)
        identf = p.sb("identf", [128, 128], F32)
        identb = p.sb("identb", [128, 128], BF16)
        onesf = p.sb("onesf", [128, 128], F32)
        maskU = p.sb("maskU", [128, 128], F32)
        Jm = p.sb("Jm", [128, 128], F32)
        mask0 = p.sb("mask0", [128, 128], F32)
        mask4 = p.sb("mask4", [128, 128], F32)
        io16 = p.sb("io16", [128, 16], F32)
        atmp = [p.sb(f"atmp{i}", [128, 128], F32) for i in range(2)]
        PTb = [p.sb(f"PT{i}", [128, 5, 128], BF16) for i in range(2)]
        oa = p.sb("oa", [128, 8, 64], F32)
        oaT = p.sb("oaT", [128, 4, 128], BF16)
        tmpf = p.sb("tmpf", [128, 4, 128], F32)
        peb = p.sb("peb", [128, 256], F32)
        peT = p.sb("peT", [128, 2, 128], BF16)
        dgm = [p.sb(f"dgm{i}", [128, GS, 128], BF16) for i in range(2)]
        kst = p.sb("kst", [128, 512], F32)
        vst = p.sb("vst", [128, 512], F32)
        cio = p.sb("cio", [128, 2, 256], F32)
        skT = p.sb("skT", [128, 16, 128], BF16)
        ifbb = p.sb("ifbb", [128, 8], F32)
        st8 = p.sb("st8", [8, 128], F32)
        dg8 = p.sb("dg8", [8, 8], F32)
        rbs = p.sb("rbs", [8, 384], F32)
        nst = p.sb("nst", [128, 8], F32)
        mnc = p.sb("mnc", [128, 8], F32)
        cb = p.sb("cb", [128, 8], F32)
        z8 = p.sb("z8", [128, 8], F32)
        e4 = p.sb("e4", [128, 4], F32)
        sp4 = p.sb("sp4", [128, 4], F32)
        rcs = p.sb("rcs", [128, 8], F32)
        rmax = p.sb("rmax", [4, 1], F32)
        dg4 = p.sb("dg4", [4, 4], F32)
        Mxb = p.sb("Mxb", [128, 4], F32)
        mb = p.sb("mb", [128, 4], F32)
        D12 = p.sb("D12", [128, 12], F32)
        E12 = p.sb("E12", [128, 12], F32)
        den = p.sb("den", [128, 2], F32)
        bst = p.sb("bst", [128, 4, 6], F32)
        bmv = p.sb("bmv", [128, 4, 2], F32)
        rstd = p.sb("rstd", [128, 4], F32)
        rs8 = p.sb("rs8", [128, 8], F32)

        PS = [p.ps(f"ps{i}", [128, 512], F32) for i in range(8)]
        PQ = [(PS[0], "ps0"), (PS[1], "ps1")]
        PTR, PSM, PY, PA1, PX0, PX1 = PS[2], PS[3], PS[4], PS[5], PS[6], PS[7]
        pq_i = [0]

        def next_pq():
            b = PQ[pq_i[0] % 2]
            pq_i[0] += 1
            return b

        qT = bq[:].rearrange("p (c l) -> p c l", c=8)
        kT = bk[:].rearrange("p (c l) -> p c l", c=8)
        x1T = kT
        hnT = bh[:].rearrange("p (c l) -> p c l", c=8)
        x2T = hnT
        m1 = fB[:].rearrange("p (c l) -> p c l", c=8)
        hbuf = fA[:].rearrange("p (h d) -> p h d", h=4)

        memset(identf[:], 1.0, ["identf"])
        p.op("pool", lambda e: e.affine_select(out=identf[:], in_=identf[:], pattern=[[-1, 128]], compare_op=ALU.is_equal,
                                              fill=0.0, base=0, channel_multiplier=1), ["identf"], ["identf"])
        memset(onesf[:], 1.0, ["onesf"])
        memset(maskU[:], 1.0, ["maskU"])
        p.op("pool", lambda e: e.affine_select(out=maskU[:], in_=maskU[:], pattern=[[1, 128]], compare_op=ALU.is_ge,
                                              fill=0.0, base=0, channel_multiplier=-1), ["maskU"], ["maskU"])
        memset(Jm[:], 1.0, ["Jm"])
        p.op("pool", lambda e: e.affine_select(out=Jm[:], in_=Jm[:], pattern=[[1, 128]], compare_op=ALU.is_equal,
                                              fill=0.0, base=-127, channel_multiplier=1), ["Jm"], ["Jm"])
        memset(mask0[:], 0.0, ["mask0"])
        memset(mask0[64:128, 0:64], NEG, ["mask0"], r=["mask0"])
        memset(mask4[:], 0.0, ["mask4"])
        memset(mask4[0:64, 64:128], NEG, ["mask4"], r=["mask4"])
        for j in range(16):
            memset(io16[:, j:j + 1], float(j), ["io16"], r=["io16"])
        vcopy(identb[:], identf[:], ["identf"], ["identb"])
        memset(D12[:], 0.0, ["D12"])
        memset(dots[:], 0.0, ["dots"])
        for i in range(5):
            memset(avr[i][:], 1.0, [f"avr{i}"])
            memset(akT[i][:], 0.0, [f"akT{i}"])
        memset(vg[:], 0.0, ["vg"])
        memset(bkt[:], 0.0, ["bkt"])

        wnames = {}

        def cast(semkey, dst, src, nm):
            lst = wnames.setdefault(nm, [])
            name = "%s#%d" % (nm, len(lst))
            lst.append(name)
            dma("pool", semkey, dst, src, [], [name])

        for l in range(DEPTH if nocast in (False, 0, 2) else 0):
            for r0 in range(0, 1024, 128):
                cast("castw", w_in_b[l, r0:r0 + 128, :], w_in[l, r0:r0 + 128, :], f"w_in_b{l}")
            for (dst, src, nm, rows) in ((wbm_b, wbm, "wbm_b", 1024), (wba_b, wba, "wba_b", 512), (wo_b, wo, "wo_b", 1024),
                                         (wq_b, wq, "wq_b", 1024), (plp_b, plp, "plp_b", 256), (plg_b, plg, "plg_b", 1024)):
                for r0 in range(0, rows, 256):
                    cast("castw", dst[l, r0:r0 + 256, :], src[l, r0:r0 + 256, :], f"{nm}{l}")
        for l in range(DEPTH if nocast in (False, 0, 3) else 0):
            pu2 = pu[l].rearrange("(a b) d -> a (b d)", b=8)
            pv2 = pv[l].rearrange("(a b) d -> a (b d)", b=8)
            ub2 = ub[l].rearrange("(a b) d -> a (b d)", b=8)
            vb2 = vb[l].rearrange("(a b) d -> a (b d)", b=8)
            for r0 in range(0, 2048, 128):
                cast(f"castt{l}", ub2[r0:r0 + 128, :], pu2[r0:r0 + 128, :], f"ub{l}")
                cast(f"castt{l}", vb2[r0:r0 + 128, :], pv2[r0:r0 + 128, :], f"vb{l}")

        stages = []

        def stage(src, kc, ncols, srcres, fn):
            stages.append((src, kc, ncols, srcres, fn))

        slab_ctr = [0]

        def load_slab(st):
            src, kc, ncols, srcres, _ = st
            if src is None:
                return None
            s = slab_ctr[0] % NSLAB
            slab_ctr[0] += 1
            t = slabs[s]
            dma("sp", f"wsl{s}", t[:, 0:kc, 0:ncols], src, wnames[srcres], [f"wsl{s}"])
            return (t, f"wsl{s}")

        def wslab(wb, l, kcn, c0, c1):
            return wb[l].rearrange("(kc p) n -> p kc n", p=128)[:, :, c0:c1]

        def proj_fm(slab, c0, nchunks, kcn, rhsT, rres, L, evac):
            sl, sres = slab
            for g0 in range(0, nchunks, 4):
                n = min(4, nchunks - g0)
                bank, bres = next_pq()
                for j in range(n):
                    for kc in range(kcn):
                        mm(bank[:, j * 128:j * 128 + L], sl[:, kc, c0 + (g0 + j) * 128:c0 + (g0 + j + 1) * 128],
                           rhsT[:, kc, :L], kc == 0, kc == kcn - 1, [sres, rres], [bres])
                view = bank[:, 0:n * 128].rearrange("p (j l) -> p j l", l=128)[:, :, :L]
                evac(view, bres, g0, n)

        def proj_tm(slab, c0, ncols, kcn, lhsT, lres, L, evac, banks=None):
            sl, sres = slab
            bi = 0
            for n0 in range(0, ncols, 512):
                n = min(512, ncols - n0)
                if banks is None:
                    bank, bres = next_pq()
                else:
                    bank, bres = banks[bi]
                    bi += 1
                for kc in range(kcn):
                    mm(bank[:L, 0:n], lhsT[:, kc, :L], sl[:, kc, c0 + n0:c0 + n0 + n], kc == 0, kc == kcn - 1,
                       [sres, lres], [bres])
                evac(bank, bres, n0, n)

        def transpose_to(dst3, dres, src2, sres, L, ncols, evac_eng="act"):
            nch = ncols // 128
            for g0 in range(0, nch, 4):
                n = min(4, nch - g0)
                for j in range(n):
                    tr(PTR[:, j * 128:j * 128 + L], src2[:L, (g0 + j) * 128:(g0 + j + 1) * 128], identf[:L, :L],
                       [sres, "identf"], ["ps2"])
                view = PTR[:, 0:n * 128].rearrange("p (j l) -> p j l", l=128)[:, :, :L]
                if evac_eng == "act":
                    acopy(dst3[:, g0:g0 + n, :L], view, ["ps2"], [dres])
                else:
                    vcopy(dst3[:, g0:g0 + n, :L], view, ["ps2"], [dres])

        def layer_norm(src, sres, dst, dres, g_b, b_b, L):
            p.op("dve", lambda e: e.bn_stats(out=bst[:L, 0, :], in_=src[:L, 0:512]), [sres], ["bst"])
            p.op("dve", lambda e: e.bn_stats(out=bst[:L, 1, :], in_=src[:L, 512:1024]), [sres], ["bst"])
            p.op("dve", lambda e: e.bn_aggr(out=bmv[:L, 0, :], in_=bst[:L, 0:2, :]), ["bst"], ["bmv"])
            act(rstd[:L, 0:1], bmv[:L, 0, 1:2], AF.Ln, ["bmv"], ["rstd"], bias=LN_EPS)
            act(rstd[:L, 0:1], rstd[:L, 0:1], AF.Exp, ["rstd"], ["rstd"], scale=-0.5)
            ts(dst[:L, :], src[:L, :], bmv[:L, 0, 0:1], rstd[:L, 0:1], ALU.subtract, ALU.mult, [sres, "bmv", "rstd"], [dres])
            tt(dst[:L, :], dst[:L, :], g_b[:L, :], ALU.mult, [dres, "lnc"], [dres])
            tt(dst[:L, :], dst[:L, :], b_b[:L, :], ALU.add, [dres, "lnc"], [dres])

        def layer_setup(l):
            dma("sp", "setup", ifbb[:], ifb[l:l + 1, :].to_broadcast([128, 8]), [], ["ifbb"])
            dma("sp", "setup", st8[0:8, :], mnw[l].rearrange("(c p) -> c p", p=128), [], ["st8"])
            tr(PSM[:, 300:308], st8[0:8, :], identf[0:8, 0:8], ["st8", "identf"], ["ps3"])
            vcopy(mnc[:, :], PSM[:, 300:308], ["ps3"], ["mnc"])
            dma("sp", "setup", rbs[0:8, 0:257], rb[l], [], ["rbs"])
            vcopy(rbs[0:8, 257:384], rbs[0:8, 256:257].to_broadcast([8, 127]), ["rbs"], ["rbs"])
            dma("sp", "setup", rbx[l], rbs[0:8, :], ["rbs"], [f"rbx{l}"])
            dma("sp", "setup", lng1[:], ln1g[l:l + 1, :].to_broadcast([128, D]), [], ["lnc"])
            dma("sp", "setup", lnb1[:], ln1b[l:l + 1, :].to_broadcast([128, D]), [], ["lnc"])
            dma("sp", "setup", lng2[:], ln2g[l:l + 1, :].to_broadcast([128, D]), [], ["lnc"])
            dma("sp", "setup", lnb2[:], ln2b[l:l + 1, :].to_broadcast([128, D]), [], ["lnc"])
            ts(dg8[0:8, 0:8], identf[0:8, 0:8], rbs[0:8, 256:257], None, ALU.mult, None, ["identf", "rbs"], ["dg8"])
            mm(PSM[:, 310:318], onesf[0:8, :], dg8[0:8, 0:8], True, True, ["onesf", "dg8"], ["ps3"])
            vcopy(cb[:, :], PSM[:, 310:318], ["ps3"], ["cb"])
            sc3 = sc[:].rearrange("p (g n) -> p g n", g=16)
            dma("sp", "setup", sc3, sk[l].rearrange("g n d -> n g d"), [], ["sc"])
            for g0 in range(0, 16, 4):
                for j in range(4):
                    tr(PTR[:, j * 128:(j + 1) * 128], sc3[:, g0 + j, :], identf[:], ["sc", "identf"], ["ps2"])
                acopy(skT[:, g0:g0 + 4, :], PTR[:].rearrange("p (j l) -> p j l", l=128), ["ps2"], ["skT"])
            for h in range(8):
                base = (l * 8 + h) * 384
                dma("sp", "setup", sc3[:, h, :], bass.AP(tensor=rbx.tensor, offset=base + 1, ap=[[1, 128], [1, 128]]),
                    [f"rbx{l}"], ["sc"])
                dma("sp", "setup", sc3[:, 8 + h, :], bass.AP(tensor=rbx.tensor, offset=base + 129, ap=[[1, 128], [1, 128]]),
                    [f"rbx{l}"], ["sc"])
            for h in range(8):
                bank, bres = next_pq()
                mm(bank[:, 0:128], Jm[:], sc3[:, h, :], True, True, ["Jm", "sc"], [bres])
                mm(bank[:, 128:256], Jm[:], sc3[:, 8 + h, :], True, True, ["Jm", "sc"], [bres])
                stt(BT0[:, h, :], bank[:, 0:128], cb[:, h:h + 1], mask0[:], ALU.subtract, ALU.add, [bres, "cb", "mask0"], ["BT0"])
                ts(BT1[:, h, :], bank[:, 128:256], cb[:, h:h + 1], None, ALU.subtract, None, [bres, "cb"], ["BT1"])

        def seq_start_prompt():
            memset(Caug[:], 0.0, ["Caug"], eng="dve")
            memset(mb[:], 0.0, ["mb"], eng="dve")

        def seq_start_sample(l, b):
            for h in range(4):
                dma("sp", "setup", cio[:], sC[l, b, h].rearrange("(vc p) k -> p vc k", p=128), [], ["cio"])
                for kc in range(2):
                    for vc in range(2):
                        tr(PTR[:, vc * 128:(vc + 1) * 128], cio[:, vc, kc * 128:(kc + 1) * 128], identf[:], ["cio", "identf"], ["ps2"])
                    acopy(Caug[:, h, kc, 0:256], PTR[:, 0:256], ["ps2"], ["Caug"])
            dma("sp", "setup", st8[0:8, :], sn[l, b].rearrange("h (c p) -> (h c) p", p=128), [], ["st8"])
            tr(PSM[:, 300:308], st8[0:8, :], identf[0:8, 0:8], ["st8", "identf"], ["ps3"])
            vcopy(Caug[:, :, :, 256], PSM[:, 300:308].rearrange("p (h c) -> p h c", h=4), ["ps3"], ["Caug"])
            dma("sp", "setup", mb[:], sm[l, b:b + 1, :].to_broadcast([128, 4]), [], ["mb"])
            for c in range(4):
                slot = (4 + c) % 5
                dma("sp", "setup", fA[:, 0:512], ck[l, b, c * 128:(c + 1) * 128, :], [], ["fA"])
                dma("sp", "setup", fA[:, 512:1024], cv[l, b, c * 128:(c + 1) * 128, :], [], ["fA"])
                for j in range(4):
                    tr(PTR[:, j * 128:(j + 1) * 128], fA[:, j * 128:(j + 1) * 128], identf[:], ["fA", "identf"], ["ps2"])
                acopy(akT[slot][:], PTR[:].rearrange("p (j l) -> p j l", l=128), ["ps2"], [f"akT{slot}"])
                vcopy(avr[slot][:, :, 0:64], fA[:, 512:1024].rearrange("p (h d) -> p h d", h=8), ["fA"], [f"avr{slot}"])

        def seq_end(l, C_dst, n_dst, m_dst):
            for h in range(4):
                for vc in range(2):
                    for kc in range(2):
                        tr(PTR[:, kc * 128:(kc + 1) * 128], Caug[:, h, kc, vc * 128:(vc + 1) * 128], identf[:],
                           ["Caug", "identf"], ["ps2"])
                    acopy(cio[:, vc, :], PTR[:, 0:256], ["ps2"], ["cio"])
                dma("act", "st", C_dst[h].rearrange("(vc p) k -> p vc k", p=128), cio[:], ["cio"], [])
            vcopy(nst[:, :].rearrange("p (h c) -> p h c", h=4), Caug[:, :, :, 256], ["Caug"], ["nst"])
            tr(PSM[0:8, 320:448], nst[:, 0:8], identf[:, :], ["nst", "identf"], ["ps3"])
            vcopy(st8[0:8, :], PSM[0:8, 320:448], ["ps3"], ["st8"])
            dma("act", "st", n_dst.rearrange("h (c p) -> (h c) p", p=128), st8[0:8, :], ["st8"], [])
            dma("act", "st", m_dst, mb[0:1, :], ["mb"], [])

        def tile(l, L, ti, xsrc, xres, xdst, xdres, pesrc, kdst, vdst):
            W = w_in_b

            def s_load():
                dma("sp", "xld", xb[:L, :], xsrc, [xres] if xres else [], ["xb"])
                dma("sp", "peld", peb[:L, :], pesrc, [], ["peb"])
                transpose_to(xT, "xT", xb, "xb", L, 1024)

            def s0(slab):
                s_load()

                def ev(view, bres, g0, n):
                    acopy(qT[:, g0:g0 + n, :L], view, [bres], ["bq"])
                proj_fm(slab, 0, 8, 8, xT, "xT", L, ev)
            stage(wslab(W, l, 8, 0, 1024), 8, 1024, f"w_in_b{l}", s0)

            def s1(slab):
                def ev(view, bres, g0, n):
                    act(kT[:, g0:g0 + n, :L], view, AF.Copy, [bres], ["bk"], scale=0.0625)
                proj_fm(slab, 0, 8, 8, xT, "xT", L, ev)

                def ev2(bank, bres, n0, n):
                    act(bkt[:L, n0:n0 + n], bank[:L, 0:n], AF.Copy, [bres], ["bkt"], scale=0.0625)
                proj_tm(slab, 0, 1024, 8, xT, "xT", L, ev2)
            stage(wslab(W, l, 8, 1024, 2048), 8, 1024, f"w_in_b{l}", s1)

            def s2(slab):
                sl, sres = slab
                for kc in range(8):
                    mm(PSM[:L, 0:8], xT[:, kc, :L], sl[:, kc, 1024:1032], kc == 0, kc == 7, [sres, "xT"], ["ps3"])
                tt(z8[:L, :], PSM[:L, 0:8], ifbb[:L, :], ALU.add, ["ps3", "ifbb"], ["z8"])
                act(e4[:L, :], z8[:L, 4:8], AF.Exp, ["z8"], ["e4"], scale=-1.0)
                act(sp4[:L, :], e4[:L, :], AF.Ln, ["e4"], ["sp4"], bias=1.0)
                mm(PSM[:L, 16:20], maskU[:L, :L], sp4[:L, :], True, True, ["maskU", "sp4"], ["ps3"])
                mm(PSM[:, 24:28], onesf[:L, :], sp4[:L, :], True, True, ["onesf", "sp4"], ["ps3"])
                tt(rcs[:L, 0:4], z8[:L, 0:4], PSM[:L, 16:20], ALU.add, ["z8", "ps3"], ["rcs"])
                vcopy(rcs[:L, 4:8], PSM[:L, 16:20], ["ps3"], ["rcs"])
                vcopy(sp4[:, :], PSM[:, 24:28], ["ps3"], ["sp4"])
                tr(PSM[0:4, 32:32 + L], rcs[:L, 0:4], identf[:L, :L], ["rcs", "identf"], ["ps3"])
                red(rmax[0:4, :], PSM[0:4, 32:32 + L], ALU.max, ["ps3"], ["rmax"])
                ts(dg4[:, :], identf[0:4, 0:4], rmax[0:4, 0:1], None, ALU.mult, None, ["identf", "rmax"], ["dg4"])
                mm(PSM[:, 200:204], onesf[0:4, :], dg4[:, :], True, True, ["onesf", "dg4"], ["ps3"])
                tt(Mxb[:, :], mb[:, :], PSM[:, 200:204], ALU.max, ["mb", "ps3"], ["Mxb"])
                tt(D12[:L, 0:4], rcs[:L, 0:4], Mxb[:L, :], ALU.subtract, ["rcs", "Mxb"], ["D12"])
                tt(D12[:, 4:8], mb[:, :], Mxb[:, :], ALU.subtract, ["mb", "Mxb"], ["D12"])
                tt(D12[:L, 8:12], rcs[:L, 4:8], Mxb[:L, :], ALU.subtract, ["rcs", "Mxb"], ["D12"])
                act(E12[:, :], D12[:, :], AF.Exp, ["D12"], ["E12"])
                tt(mb[:, :], Mxb[:, :], sp4[:, :], ALU.subtract, ["Mxb", "sp4", "D12"], ["mb"])

                def ev(bank, bres, n0, n):
                    for hh in range(2):
                        h = n0 // 256 + hh
                        ts(vg[:L, h, 0:256], bank[:L, hh * 256:(hh + 1) * 256], E12[:L, h:h + 1], None, ALU.mult, None,
                           [bres, "E12"], ["vg"])
                proj_tm(slab, 0, 1024, 8, xT, "xT", L, ev)
                vcopy(vg[:L, :, 256], E12[:L, 0:4], ["E12"], ["vg"])
            stage(wslab(W, l, 8, 2048, 3080), 8, 1032, f"w_in_b{l}", s2)

            def s3(slab):
                def ev(view, bres, g0, n):
                    act(sgmo[:, g0:g0 + n, :L], view, AF.Sigmoid, [bres], ["sgmo"])
                proj_fm(slab, 0, 8, 8, xT, "xT", L, ev)
                for h in range(4):
                    AT = ATb[h % 2]; ares = f"AT{h % 2}"
                    Cb = Csb[h % 2]; cres = f"Csb{h % 2}"
                    for c in range(2):
                        mm(PA1[:L, 0:L], kT[:, 2 * h + c, :L], qT[:, 2 * h + c, :L], c == 0, c == 1, ["bk", "bq"], ["ps5"])
                    tt(AT[:L, :L], PA1[:L, 0:L], maskU[:L, :L], ALU.mult, ["ps5", "maskU"], [ares])
                    ts(Cb[:, :, :], Caug[:, h, :, :], E12[:, 4 + h:5 + h], None, ALU.mult, None, ["Caug", "E12"], [cres])
                    mm(PY[:L, 0:257], qT[:, 2 * h, :L], Cb[:, 0, :], True, False, ["bq", cres], ["ps4"])
                    mm(PY[:L, 0:257], qT[:, 2 * h + 1, :L], Cb[:, 1, :], False, False, ["bq", cres], ["ps4"])
                    mm(PY[:L, 0:257], AT[:L, :L], vg[:L, h, :], False, True, [ares, "vg"], ["ps4"])
                    ts(den[:L, 0:1], PY[:L, 256:257], -1.0, None, ALU.mult, None, ["ps4"], ["den"])
                    tt(den[:L, 0:1], den[:L, 0:1], PY[:L, 256:257], ALU.max, ["den", "ps4"], ["den"])
                    tt(den[:L, 0:1], den[:L, 0:1], E12[:L, 8 + h:9 + h], ALU.max, ["den", "E12"], ["den"])
                    recip(den[:L, 1:2], den[:L, 0:1], ["den"], ["den"])
                    ts(hbuf[:L, h, :], PY[:L, 0:256], den[:L, 1:2], None, ALU.mult, None, ["ps4", "den"], ["fA"])
                    for c in range(2):
                        bank, bres = (PX0, "ps6") if c == 0 else (PX1, "ps7")
                        mm(bank[:, 0:257], bkt[:L, h * 256 + c * 128:h * 256 + (c + 1) * 128], vg[:L, h, :], True, True,
                           ["bkt", "vg"], [bres])
                        stt(Caug[:, h, c, :], Caug[:, h, c, :], E12[:, 4 + h:5 + h], bank[:, 0:257], ALU.mult, ALU.add,
                            ["Caug", "E12", bres], ["Caug"])
                for h in range(4):
                    p.op("dve", lambda e, h=h: e.bn_stats(out=bst[:L, h, :], in_=hbuf[:L, h, :]), ["fA"], ["bst"])
                    p.op("dve", lambda e, h=h: e.bn_aggr(out=bmv[:L, h, :], in_=bst[:L, h:h + 1, :]), ["bst"], ["bmv"])
                act(rstd[:L, :], bmv[:L, :, 1], AF.Ln, ["bmv"], ["rstd"], bias=LN_EPS)
                act(rstd[:L, :], rstd[:L, :], AF.Exp, ["rstd"], ["rstd"], scale=-0.5)
                for h in range(4):
                    ts(hbuf[:L, h, :], hbuf[:L, h, :], bmv[:L, h, 0:1], rstd[:L, h:h + 1], ALU.subtract, ALU.mult,
                       ["fA", "bmv", "rstd"], ["fA"])
                for g0 in (0, 4):
                    for j in range(4):
                        tr(PTR[:, j * 128:j * 128 + L], fA[:L, (g0 + j) * 128:(g0 + j + 1) * 128], identf[:L, :L],
                           ["fA", "identf"], ["ps2"])
                    view = PTR[:].rearrange("p (j l) -> p j l", l=128)[:, :, :L]
                    tt(tmpf[:, :, :L], view, mnc[:, g0:g0 + 4, None].to_broadcast([128, 4, L]), ALU.mult, ["ps2", "mnc"], ["tmpf"])
                    tt(hnT[:, g0:g0 + 4, :L], tmpf[:, :, :L], sgmo[:, g0:g0 + 4, :L], ALU.mult, ["tmpf", "sgmo"], ["bh"])
            stage(wslab(W, l, 8, 3080, 4104), 8, 1024, f"w_in_b{l}", s3)

            slot_own = ti % 5

            def s4(slab):
                def ev(view, bres, g0, n):
                    act(aqT[:, :, :L], view, AF.Copy, [bres], ["aqT"], scale=0.125)
                proj_fm(slab, 0, 4, 8, xT, "xT", L, ev)

                def ev2(view, bres, g0, n):
                    acopy(akT[slot_own][:, :, :L], view, [bres], [f"akT{slot_own}"])
                proj_fm(slab, 512, 4, 8, xT, "xT", L, ev2)
                if kdst is not None:
                    def ev3(bank, bres, n0, n):
                        acopy(kst[:L, :], bank[:L, 0:512], [bres], ["kst"])
                    proj_tm(slab, 512, 512, 8, xT, "xT", L, ev3)
                    dma("act", "st", kdst, kst[:L, :], ["kst"], [])
            stage(wslab(W, l, 8, 4104, 5128), 8, 1024, f"w_in_b{l}", s4)

            def s5(slab):
                def ev(bank, bres, n0, n):
                    acopy(avr[slot_own][:L, :, 0:64], bank[:L, 0:512].rearrange("p (h d) -> p h d", h=8), [bres], [f"avr{slot_own}"])
                    if vdst is not None:
                        vcopy(vst[:L, :], bank[:L, 0:512], [bres], ["vst"])
                proj_tm(slab, 0, 512, 8, xT, "xT", L, ev)
                if vdst is not None:
                    dma("act", "st", vdst, vst[:L, :], ["vst"], [])
                tiles = [(ti - dlt, dlt) for dlt in (4, 3, 2, 1, 0) if ti - dlt >= 0]
                for h in range(8):
                    c = h // 2
                    pb = (h % 2) * 64
                    PT = PTb[h % 2]; pres = f"PT{h % 2}"
                    obank, obres = (PX0, "ps6") if h < 4 else (PX1, "ps7")
                    for (j, dlt) in tiles:
                        Lk = L if dlt == 0 else 128
                        sl_ = j % 5
                        if dlt == 0:
                            dst = PA1[:Lk, 0:L]; dres = "ps5"
                        else:
                            dst = PY[:Lk, (4 - dlt) * 128:(4 - dlt) * 128 + L]; dres = "ps4"
                        mm(dst, akT[sl_][pb:pb + 64, c, :Lk], aqT[pb:pb + 64, c, :L], True, True, [f"akT{sl_}", "aqT"], [dres])
                    for ii, (j, dlt) in enumerate(tiles):
                        Lk = L if dlt == 0 else 128
                        if dlt == 0:
                            src = PA1[:Lk, 0:L]; sres = "ps5"
                        else:
                            src = PY[:Lk, (4 - dlt) * 128:(4 - dlt) * 128 + L]; sres = "ps4"
                        if dlt in (2, 3):
                            act(PT[:Lk, 4 - dlt, :L], src, AF.Exp, [sres], [pres])
                        else:
                            tmp = atmp[ii % 2]; tres = f"atmp{ii % 2}"
                            if dlt == 0:
                                tt(tmp[:Lk, :L], src, BT0[:Lk, h, :L], ALU.add, [sres, "BT0"], [tres])
                            elif dlt == 1:
                                tt(tmp[:Lk, :L], src, BT1[:Lk, h, :L], ALU.add, [sres, "BT1"], [tres])
                            else:
                                tt(tmp[:Lk, :L], src, mask4[:Lk, :L], ALU.add, [sres, "mask4"], [tres])
                            act(PT[:Lk, 4 - dlt, :L], tmp[:Lk, :L], AF.Exp, [tres], [pres])
                    for ii, (j, dlt) in enumerate(tiles):
                        Lk = L if dlt == 0 else 128
                        sl_ = j % 5
                        mm(obank[:L, (h % 4) * 65:(h % 4) * 65 + 65], PT[:Lk, 4 - dlt, :L], avr[sl_][:Lk, h, :],
                           ii == 0, ii == len(tiles) - 1, [pres, f"avr{sl_}"], [obres])
                for half, (obank, obres) in enumerate(((PX0, "ps6"), (PX1, "ps7"))):
                    ov = obank[:L, 0:260].rearrange("p (h d) -> p h d", h=4)
                    recip(rs8[:L, half * 4:half * 4 + 4], ov[:, :, 64], [obres], ["rs8"])
                    tt(oa[:L, half * 4:half * 4 + 4, :], ov[:, :, 0:64],
                       rs8[:L, half * 4:half * 4 + 4, None].to_broadcast([L, 4, 64]), ALU.mult, [obres, "rs8"], ["oa"])
                transpose_to(oaT, "oaT", oa[:].rearrange("p h d -> p (h d)"), "oa", L, 512)
            stage(wslab(W, l, 8, 5128, 5640), 8, 512, f"w_in_b{l}", s5)

            def s6(slab):
                def ev(view, bres, g0, n):
                    act(sggm[:, g0:g0 + n, :L], view, AF.Sigmoid, [bres], ["sggm"])
                proj_fm(slab, 0, 8, 8, xT, "xT", L, ev)
            stage(wslab(W, l, 8, 5640, 6664), 8, 1024, f"w_in_b{l}", s6)

            def s7(slab):
                def ev(view, bres, g0, n):
                    act(sgga[:, g0:g0 + n, :L], view, AF.Sigmoid, [bres], ["sgga"])
                proj_fm(slab, 0, 8, 8, xT, "xT", L, ev)
            stage(wslab(W, l, 8, 6664, 7688), 8, 1024, f"w_in_b{l}", s7)

            def s8(slab):
                def ev(view, bres, g0, n):
                    tt(m1[:, g0:g0 + n, :L], view, sggm[:, g0:g0 + n, :L], ALU.mult, [bres, "sggm"], ["fB"])
                proj_fm(slab, 0, 8, 8, hnT, "bh", L, ev)
            stage(wslab(wbm_b, l, 8, 0, 1024), 8, 1024, f"wbm_b{l}", s8)

            def s9(slab):
                def ev(view, bres, g0, n):
                    tt(tmpf[:, 0:n, :L], view, sgga[:, g0:g0 + n, :L], ALU.mult, [bres, "sgga"], ["tmpf"])
                    tt(mixT[:, g0:g0 + n, :L], tmpf[:, 0:n, :L], m1[:, g0:g0 + n, :L], ALU.add, ["tmpf", "fB"], ["mixT"])
                proj_fm(slab, 0, 8, 4, oaT, "oaT", L, ev)
            stage(wslab(wba_b, l, 4, 0, 1024), 4, 1024, f"wba_b{l}", s9)

            def s10(slab):
                def ev(bank, bres, n0, n):
                    stt(fA[:L, n0:n0 + n], xb[:L, n0:n0 + n], ALPHA, bank[:L, 0:n], ALU.mult, ALU.add, ["xb", bres], ["fA"])
                proj_tm(slab, 0, 1024, 8, mixT, "mixT", L, ev)
                layer_norm(fA, "fA", x1, "x1", lng1, lnb1, L)
                dump(f"x1_{l}_{ti}_{L}", x1[:L, :], "x1", [L, D])
                transpose_to(x1T, "bk", x1, "x1", L, 1024)
                vcopy(bq[:L, :], x1[:L, :], ["x1"], ["bq"])
            stage(wslab(wo_b, l, 8, 0, 1024), 8, 1024, f"wo_b{l}", s10)

            def s11(slab):
                def ev(view, bres, g0, n):
                    acopy(qrT[:, g0:g0 + n, :L], view, [bres], ["qrT"])
                proj_fm(slab, 0, 8, 8, x1T, "bk", L, ev)
            stage(wslab(wq_b, l, 8, 0, 1024), 8, 1024, f"wq_b{l}", s11)

            def s12(slab):
                def ev(view, bres, g0, n):
                    acopy(qrT[:, 8 + g0:8 + g0 + n, :L], view, [bres], ["qrT"])
                proj_fm(slab, 0, 8, 8, x1T, "bk", L, ev)
                peer(l, L)
                layer_norm(fA, "fA", fB, "fB", lng2, lnb2, L)
                transpose_to(x2T, "bh", fB, "fB", L, 1024)
                transpose_to(peT, "peT", peb, "peb", L, 256)
            stage(wslab(wq_b, l, 8, 1024, 2048), 8, 1024, f"wq_b{l}", s12)

            def s13(slab):
                def ev(bank, bres, n0, n):
                    pass
                proj_tm(slab, 0, 1024, 2, peT, "peT", L, ev, banks=[(PS[0], "ps0"), (PS[1], "ps1")])
            stage(wslab(plp_b, l, 2, 0, 1024), 2, 1024, f"plp_b{l}", s13)

            def s14(slab):
                def ev(bank, bres, n0, n):
                    act(fA[:L, n0:n0 + n], bank[:L, 0:n], AF.Sigmoid, [bres], ["fA"])
                proj_tm(slab, 0, 1024, 8, x2T, "bh", L, ev, banks=[(PX0, "ps6"), (PX1, "ps7")])
                for half in range(2):
                    n0 = half * 512
                    tt(fA[:L, n0:n0 + 512], fA[:L, n0:n0 + 512], PS[half][:L, 0:512], ALU.mult, ["fA", f"ps{half}"], ["fA"])
                tt(fA[:L, :], fA[:L, :], fB[:L, :], ALU.add, ["fA", "fB"], ["fA"])
                dma("act", "st", xdst, fA[:L, :], ["fA"], [xdres] if xdres else [])
            stage(wslab(plg_b, l, 8, 0, 1024), 8, 1024, f"plg_b{l}", s14)

        def peer(l, L):
            sc3 = sc[:].rearrange("p (g n) -> p g n", g=16)
            banks = [(PS[0], "ps0"), (PS[1], "ps1"), (PY, "ps4"), (PA1, "ps5")]
            memset(dots[:L, :], 0.0, ["dots"], eng="dve")
            for b4 in range(4):
                bank, bres = banks[b4]
                for j in range(4):
                    g = b4 * 4 + j
                    mm(bank[:L, j * 128:(j + 1) * 128], qrT[:, g, :L], skT[:, g, :], True, True, ["qrT", "skT"], [bres])
                acopy(sc[:L, b4 * 512:(b4 + 1) * 512], bank[:L, :], [bres], ["sc"])
            for g in range(16):
                s2 = sc2[g % 2]; s2r = f"sc2_{g % 2}"
                p.op("dve", lambda e, g=g: e.max(out=top[:L, g, 0:8], in_=sc3[:L, g, :]), ["sc"], ["top"])
                p.op("dve", lambda e, g=g: e.max_index(out=idxu[:L, g, 0:8], in_max=top[:L, g, 0:8], in_values=sc3[:L, g, :]),
                     ["sc", "top"], ["idxu"])
                p.op("dve", lambda e, g=g, s2=s2: e.match_replace(out=s2[:L, :], in_to_replace=top[:L, g, 0:8],
                                                                 in_values=sc3[:L, g, :], imm_value=-1e30), ["sc", "top"], [s2r])
                p.op("dve", lambda e, g=g, s2=s2: e.max(out=top[:L, g, 8:16], in_=s2[:L, :]), [s2r], ["top"])
                p.op("dve", lambda e, g=g, s2=s2: e.max_index(out=idxu[:L, g, 8:16], in_max=top[:L, g, 8:16], in_values=s2[:L, :]),
                     [s2r, "top"], ["idxu"])
            vcopy(sif[:L], idxu[:L], ["idxu"], ["sif"])
            topv = top[:].rearrange("p (h t) k -> p h t k", t=2)
            sifv = sif[:].rearrange("p (h t) k -> p h t k", t=2)
            cand = sc[:].rearrange("p (h a b) -> p h a b", h=8, a=16)
            cand3 = sc[:].rearrange("p (h n) -> p h n", h=8)
            tt(cand[:L], topv[:L, :, 0, :, None].to_broadcast([L, 8, 16, 16]), topv[:L, :, 1, None, :].to_broadcast([L, 8, 16, 16]),
               ALU.add, ["top"], ["sc"])
            for h in range(8):
                c2 = cand2[h % 2]; c2r = f"cand2_{h % 2}"
                p.op("dve", lambda e, h=h: e.max(out=fv[:L, h, 0:8], in_=cand3[:L, h, :]), ["sc"], ["fv"])
                p.op("dve", lambda e, h=h: e.max_index(out=fiu[:L, h, 0:8], in_max=fv[:L, h, 0:8], in_values=cand3[:L, h, :]),
                     ["sc", "fv"], ["fiu"])
                p.op("dve", lambda e, h=h, c2=c2: e.match_replace(out=c2[:L, :], in_to_replace=fv[:L, h, 0:8],
                                                                 in_values=cand3[:L, h, :], imm_value=-1e30), ["sc", "fv"], [c2r])
                p.op("dve", lambda e, h=h, c2=c2: e.max(out=fv[:L, h, 8:16], in_=c2[:L, :]), [c2r], ["fv"])
                p.op("dve", lambda e, h=h, c2=c2: e.max_index(out=fiu[:L, h, 8:16], in_max=fv[:L, h, 8:16], in_values=c2[:L, :]),
                     [c2r, "fv"], ["fiu"])
            tsc(fau[:L], fiu[:L], 4, ALU.logical_shift_right, ["fiu"], ["fau"])
            tsc(fbu[:L], fiu[:L], 15, ALU.bitwise_and, ["fiu"], ["fbu"])
            vcopy(faf[:L], fau[:L], ["fau"], ["faf"])
            vcopy(fbf[:L], fbu[:L], ["fbu"], ["fbf"])
            for (ff, fres, t_, sel, sres) in ((faf, "faf", 0, sel0, "sel0"), (fbf, "fbf", 1, sel1, "sel1")):
                tt(cand[:L], ff[:L, :, :, None].to_broadcast([L, 8, 16, 16]), io16[:L, None, None, :].to_broadcast([L, 8, 16, 16]),
                   ALU.is_equal, [fres, "io16", "fiu"], ["sc"])
                tt(cand[:L], cand[:L], sifv[:L, :, t_, None, :].to_broadcast([L, 8, 16, 16]), ALU.mult, ["sc", "sif"], ["sc"])
                red(sel[:L], cand[:L], ALU.add, ["sc"], [sres])
            stt(eidxf[:L, :], sel0[:L].rearrange("p h k -> p (h k)"), 128.0, sel1[:L].rearrange("p h k -> p (h k)"),
                ALU.mult, ALU.add, ["sel0", "sel1"], ["eidxf"])
            ts(eidxf[:L, :], eidxf[:L, :], 0.0, 16383.0, ALU.max, ALU.min, ["eidxf"], ["eidxf"])
            vcopy(eidxi[:L, :], eidxf[:L, :], ["eidxf"], ["eidxi"])
            tt(gw[:L], fv[:L], fv[:L, :, 0:1].to_broadcast([L, 8, 16]), ALU.subtract, ["fv"], ["gw"])
            act(gw[:L], gw[:L], AF.Exp, ["gw"], ["gw"])
            red(gsum[:L], gw[:L], ALU.add, ["gw"], ["gsum"])
            recip(gsum[:L], gsum[:L], ["gsum"], ["gsum"])
            tt(gw[:L], gw[:L], gsum[:L, :, None].to_broadcast([L, 8, 16]), ALU.mult, ["gw", "gsum"], ["gw"])
            nch = 128 // GS
            gctr = peer.gctr
            for ci in range(nch):
                k = gctr[0] % NGB
                gctr[0] += 1
                gbuf = gbs[k]
                for s in range(GS):
                    j = ci * GS + s
                    p.dma("pool", f"gq{k}", lambda e, gbuf=gbuf, s=s, j=j: e.indirect_dma_start(
                        out=gbuf[:L, s, :], out_offset=None, in_=ub[l],
                        in_offset=bass.IndirectOffsetOnAxis(ap=eidxi[:L, j:j + 1], axis=0)),
                        wnames[f"ub{l}"] + ["eidxi", f"gbdone{k}"], [f"gb{k}_{s}"], group=True, nowaw=True)
                for s in range(GS):
                    j = ci * GS + s
                    stt(bkt[:L, :], gbuf[:L, s, :], 1.0, bq[:L, :], ALU.mult, ALU.mult, [f"gb{k}_{s}", "bq"],
                        ["bkt", "dots"] + ([f"gbdone{k}"] if s == GS - 1 else []), accum=dots[:L, j:j + 1])
            tt(ge1[:L], dots[:L], dots[:L], ALU.mult, ["dots"], ["ge1"])
            ts(ge1[:L], ge1[:L], 0.044715, 1.0, ALU.mult, ALU.add, ["ge1"], ["ge1"])
            tt(ge1[:L], ge1[:L], dots[:L], ALU.mult, ["ge1", "dots"], ["ge1"])
            act(ge2[:L], ge1[:L], AF.Sigmoid, ["ge1"], ["ge2"], scale=1.5957691216057308)
            tt(ge2[:L], ge2[:L], dots[:L], ALU.mult, ["ge2", "dots"], ["ge2"])
            tt(wgt[:L], ge2[:L], gw[:L].rearrange("p h k -> p (h k)"), ALU.mult, ["ge2", "gw"], ["wgt"])
            for ci in range(nch):
                k = gctr[0] % NGB
                gctr[0] += 1
                gbuf = gbs[k]
                dg = dgm[ci % 2]; dres = f"dgm{ci % 2}"
                for s in range(GS):
                    j = ci * GS + s
                    p.dma("pool", f"gq{k}", lambda e, gbuf=gbuf, s=s, j=j: e.indirect_dma_start(
                        out=gbuf[:L, s, :], out_offset=None, in_=vb[l],
                        in_offset=bass.IndirectOffsetOnAxis(ap=eidxi[:L, j:j + 1], axis=0)),
                        wnames[f"vb{l}"] + ["eidxi", f"gbdone{k}"], [f"gb{k}_{s}"], group=True, nowaw=True)
                tt(dg[:L, :, :L], identb[:L, None, :L].to_broadcast([L, GS, L]),
                   wgt[:L, ci * GS:(ci + 1) * GS, None].to_broadcast([L, GS, L]), ALU.mult, ["identb", "wgt"], [dres])
                for s in range(GS):
                    j = ci * GS + s
                    for n in range(2):
                        bank, bres = (PX0, "ps6") if n == 0 else (PX1, "ps7")
                        mm(bank[:L, 0:512], dg[:L, s, :L], gbuf[:L, s, n * 512:(n + 1) * 512], j == 0, j == 127,
                           [dres, f"gb{k}_{s}"], [bres] + ([f"gbdone{k}"] if (s == GS - 1 and n == 1) else []))
            for n in range(2):
                bank, bres = (PX0, "ps6") if n == 0 else (PX1, "ps7")
                stt(fA[:L, n * 512:(n + 1) * 512], x1[:L, n * 512:(n + 1) * 512], ALPHA, bank[:L, 0:512], ALU.mult, ALU.add,
                    ["x1", bres], ["fA"])
        peer.gctr = [0]

        for l in range(DEPTH):
            stage(None, 0, 0, None, lambda slab, l=l: layer_setup(l))
            for s in range(NP):
                stage(None, 0, 0, None, lambda slab: seq_start_prompt())
                for t in range(NT):
                    xsrc = (xp if l == 0 else xmp)[s, t * 128:(t + 1) * 128, :]
                    xres = None if l == 0 else f"xmp_{s}_{t}"
                    xdst = (xmp if l == 0 else yp)[s, t * 128:(t + 1) * 128, :]
                    xdres = f"xmp_{s}_{t}" if l == 0 else None
                    kd = vd = None
                    if t >= NT - 4:
                        r0 = (t - (NT - 4)) * 128
                        kd = kp_o[l, s, r0:r0 + 128, :]
                        vd = vp_o[l, s, r0:r0 + 128, :]
                    tile(l, 128, t, xsrc, xres, xdst, xdres, pp[l, s, t * 128:(t + 1) * 128, :], kd, vd)
                stage(None, 0, 0, None, lambda slab, l=l, s=s: seq_end(l, Cp_o[l, s], np_o[l, s], mp_o[l, s:s + 1, :]))
            for b in range(NS):
                stage(None, 0, 0, None, lambda slab, l=l, b=b: seq_start_sample(l, b))
                xsrc = (xs if l == 0 else xms)[b]
                xres = None if l == 0 else f"xms_{b}"
                xdst = (xms if l == 0 else ys)[b]
                xdres = f"xms_{b}" if l == 0 else None
                tile(l, 64, 8, xsrc, xres, xdst, xdres, psm[l, b], ks_o[l, b], vs_o[l, b])
                stage(None, 0, 0, None, lambda slab, l=l, b=b: seq_end(l, Cs_o[l, b], ns_o[l, b], ms_o[l, b:b + 1, :]))

        nxt_i = 0
        pending = {}

        def prefetch_upto(i):
            nonlocal nxt_i
            while nxt_i <= i and nxt_i < len(stages):
                pending[nxt_i] = load_slab(stages[nxt_i])
                nxt_i += 1

        if limit is not None:
            stages = stages[:limit]
        for i, st in enumerate(stages):
            prefetch_upto(i)
            j = i + 1
            while j < len(stages) and stages[j][0] is None:
                j += 1
            if st[0] is not None:
                prefetch_upto(j)
            st[4](pending.pop(i))

        p.final_wait("sp")
        p.emit()
    return nc, dbg_outs


def make_in_maps(inputs, n_cores, NP, NS):
    f = lambda a: np.ascontiguousarray(np.asarray(a, dtype=np.float32))
    maps = []
    for c in range(n_cores):
        ps_ = slice(c * NP, (c + 1) * NP)
        ss = slice(c * NS, (c + 1) * NS)
        m = {
            "xp": f(inputs["x_prompt"][ps_]),
            "xs": f(inputs["x_sample"][ss]),
            "ck": f(np.asarray(inputs["cache_attn_k"])[:, ss].reshape(DEPTH, NS, 512, 512)),
            "cv": f(np.asarray(inputs["cache_attn_v"])[:, ss].reshape(DEPTH, NS, 512, 512)),
            "sC": f(np.asarray(inputs["state_mlstm_C"])[:, ss]),
            "sn": f(np.asarray(inputs["state_mlstm_n"])[:, ss]),
            "sm": f(np.asarray(inputs["state_mlstm_m"])[:, ss]),
            "pp": f(np.asarray(inputs["p_prompt"])[:, ps_]),
            "psm": f(np.asarray(inputs["p_sample"])[:, ss]),
            "w_in": f(inputs["w_in"]),
            "ifb": f(np.asarray(inputs["mlstm_if_bias"]).reshape(DEPTH, 8)),
            "mnw": f(np.asarray(inputs["mlstm_norm_w"]).reshape(DEPTH, 1024)),
            "rb": f(inputs["attn_rel_bias"]),
            "wbm": f(inputs["w_branch_m"]), "wba": f(inputs["w_branch_a"]), "wo": f(inputs["w_out"]),
            "ln1g": f(inputs["ln1_g"]), "ln1b": f(inputs["ln1_b"]), "ln2g": f(inputs["ln2_g"]), "ln2b": f(inputs["ln2_b"]),
            "wq": f(inputs["peer_wq"]),
            "sk": f(np.asarray(inputs["peer_subkeys"]).reshape(DEPTH, 16, 128, 128)),
            "pu": f(inputs["peer_u"]), "pv": f(inputs["peer_v"]),
            "plp": f(inputs["ple_proj"]), "plg": f(inputs["ple_gate"]),
        }
        maps.append(m)
    return maps


def assemble(results, NP, NS):
    cat = lambda k, ax: np.concatenate([np.asarray(r[k]) for r in results], axis=ax)
    yp = cat("yp", 0); ys = cat("ys", 0)
    kp = cat("kp", 1); vp = cat("vp", 1)
    kp = kp.reshape(kp.shape[0], kp.shape[1], kp.shape[2], 8, 64); vp = vp.reshape(kp.shape)
    Cp = cat("Cp", 1); npp = cat("np", 1); mp = cat("mp", 1)
    ks = cat("ks", 1); vs = cat("vs", 1)
    ks = ks.reshape(ks.shape[0], ks.shape[1], ks.shape[2], 8, 64); vs = vs.reshape(ks.shape)
    Cs = cat("Cs", 1); ns = cat("ns", 1); ms = cat("ms", 1)
    return tuple(np.ascontiguousarray(a, dtype=np.float32) for a in (yp, ys, kp, vp, Cp, npp, mp, ks, vs, Cs, ns, ms))


def kernel(**inputs):
    n_cores = 8
    B = inputs["x_prompt"].shape[0]
    SEQ = inputs["x_prompt"].shape[1]
    BS = inputs["x_sample"].shape[0]
    NP = B // n_cores
    NS = BS // n_cores
    nc, _ = build(NP, SEQ, NS)
    maps = make_in_maps(inputs, n_cores, NP, NS)
    res = run_bass_kernel_spmd(nc, maps, core_ids=list(range(n_cores)))
    return assemble(res.results, NP, NS)
```
